# Optimizing a Trainium2 kernel written in Bass

```python
import math
import jax, jax.numpy as jnp
from jax import lax
import numpy as np

D_MODEL = 1024
BATCH = 8
SEQ = 2048
DEPTH = 1

RET_HEADS = 4
RET_DK = 256
RET_DV = 512
RET_CHUNK = 128
RET_QK_W = RET_HEADS * RET_DK
RET_V_W = RET_HEADS * RET_DV
ATT_GROUPS = ((128, 1), (512, 4), (2048, 16))
N_ATT_GROUPS = len(ATT_GROUPS)
ATT_HEADS_PER_GROUP = 4
ATT_HEAD_DIM = 128
ATT_GROUP_W = ATT_HEADS_PER_GROUP * ATT_HEAD_DIM
N_ATT_HEADS = N_ATT_GROUPS * ATT_HEADS_PER_GROUP
REL_BUCKETS = 32
REL_MAX_DIST = 2048
D_FF = 4 * D_MODEL
N_BRANCHES = 2
RMS_EPS = 1e-6
GN_EPS = 1e-5
ROPE_BASE = 10000.0

IN_SIZES = ([RET_QK_W, RET_QK_W, RET_V_W, RET_V_W]
            + [ATT_GROUP_W] * (3 * N_ATT_GROUPS)
            + [D_MODEL] * N_BRANCHES)
IN_COLS = sum(IN_SIZES)
IN_OFFSETS = [sum(IN_SIZES[:i + 1]) for i in range(len(IN_SIZES) - 1)]

kernel_name = "hybrid_retention_dilated_attn_block"


def rms_norm(x, g):
    xf = x.astype(jnp.float32)
    y = xf * lax.rsqrt(jnp.mean(xf * xf, axis=-1, keepdims=True) + RMS_EPS)
    return (y * g.astype(jnp.float32)).astype(x.dtype)


def modulate(h, shift, scale):
    return h * (1 + scale[:, None, :]) + shift[:, None, :]


def t5_bucket(dist):
    max_exact = REL_BUCKETS // 2
    d_f = jnp.maximum(dist, 1).astype(jnp.float32)
    large = max_exact + (jnp.log(d_f / max_exact) / math.log(REL_MAX_DIST / max_exact)
                         * (REL_BUCKETS - max_exact)).astype(jnp.int32)
    large = jnp.minimum(large, REL_BUCKETS - 1)
    return jnp.where(dist < max_exact, dist, large)


def rotary(x, pos):
    half = x.shape[-1] // 2
    inv = ROPE_BASE ** (-jnp.arange(half, dtype=jnp.float32) / half)
    ang = pos.astype(jnp.float32)[:, None] * inv[None, :]
    cos, sin = jnp.cos(ang).astype(x.dtype), jnp.sin(ang).astype(x.dtype)
    x1, x2 = x[..., :half], x[..., half:]
    return jnp.concatenate([x1 * cos - x2 * sin, x1 * sin + x2 * cos], axis=-1)


def retention(q, k, v):
    B, H, S, dk = q.shape
    dv = v.shape[-1]
    C = RET_CHUNK
    nc = S // C
    log_g = jnp.log1p(-(2.0 ** (-5.0 - jnp.arange(H, dtype=jnp.float32))))
    idx = jnp.arange(C, dtype=jnp.float32)
    rel = idx[:, None] - idx[None, :]
    inner_decay = jnp.where(rel >= 0, jnp.exp(log_g[:, None, None] * jnp.maximum(rel, 0.0)), 0.0)
    q_decay = jnp.exp(log_g[:, None] * (idx + 1.0))
    k_decay = jnp.exp(log_g[:, None] * (C - 1.0 - idx))
    chunk_decay = jnp.exp(log_g * C)

    def to_chunks(t):
        return jnp.moveaxis(t.astype(jnp.float32).reshape(B, H, nc, C, t.shape[-1]), 2, 0)

    qc, kc, vc = to_chunks(q), to_chunks(k), to_chunks(v)

    def step(state, inp):
        qi, ki, vi = inp
        s = jnp.einsum('bhid,bhjd->bhij', qi, ki) * inner_decay[None]
        o = (jnp.einsum('bhij,bhje->bhie', s, vi)
             + jnp.einsum('bhid,bhde->bhie', qi, state) * q_decay[None, :, :, None])
        state = (state * chunk_decay[None, :, None, None]
                 + jnp.einsum('bhjd,bhje->bhde', ki * k_decay[None, :, :, None], vi))
        return state, o

    state0 = jnp.zeros((B, H, dk, dv), jnp.float32)
    _, o = lax.scan(step, state0, (qc, kc, vc))
    return jnp.moveaxis(o, 0, 2).reshape(B, H, S, dv)


def dilated_group(q, k, v, bias_tab, window, dilation):
    B, H, S, dh = q.shape
    w = window // dilation
    blk = w
    span = dilation * blk
    Sp = -(-S // span) * span
    L = Sp // dilation
    nb = L // blk

    def split(t):
        t = jnp.pad(t, ((0, 0), (0, 0), (0, Sp - S), (0, 0)))
        t = t.reshape(B, H, L, dilation, dh).transpose(0, 1, 3, 2, 4)
        return t.reshape(B, H, dilation, nb, blk, dh)

    qs, ks, vs = split(q), split(k), split(v)

    def with_prev(t):
        prev = jnp.pad(t[:, :, :, :-1], ((0, 0), (0, 0), (0, 0), (1, 0), (0, 0), (0, 0)))
        return jnp.concatenate([prev, t], axis=4)

    kb, vb = with_prev(ks), with_prev(vs)
    qi = jnp.arange(blk)[:, None]
    kj = jnp.arange(2 * blk)[None, :]
    m = blk + qi - kj
    band = (m >= 0) & (m <= w)
    first_ok = kj >= blk
    valid = band[None] & ((jnp.arange(nb)[:, None, None] > 0) | first_ok[None])
    bias = bias_tab[t5_bucket(jnp.clip(m, 0, w) * dilation)]
    bias = jnp.moveaxis(bias, -1, 0).astype(jnp.float32)

    s = (jnp.einsum('bhrnid,bhrnjd->bhrnij', qs, kb).astype(jnp.float32) * (dh ** -0.5)
         + bias[None, :, None, None])
    s = jnp.where(valid[None, None, None], s, -1e30)
    mx = jnp.max(s, axis=-1, keepdims=True)
    e = jnp.exp(s - mx)
    den = jnp.sum(e, axis=-1, keepdims=True)
    p = (e / den).astype(v.dtype)
    lse = (mx + jnp.log(den))[..., 0]
    o = jnp.einsum('bhrnij,bhrnjd->bhrnid', p, vb)

    def merge(t):
        t = t.reshape(B, H, dilation, L, *t.shape[5:])
        t = jnp.swapaxes(t, 2, 3)
        t = t.reshape(B, H, Sp, *t.shape[4:])
        return t[:, :, :S]

    return merge(o), merge(lse)


def token_mixer(h, w_in, rel_bias, gn_g, gn_b, w_ret_out, w_att_out, w_o):
    B, S, _ = h.shape
    proj = h @ w_in
    parts = jnp.split(proj, IN_OFFSETS, axis=-1)

    def heads(t, n):
        return t.reshape(B, S, n, -1).transpose(0, 2, 1, 3)

    pos = jnp.arange(S)
    rq, rk, rv, rg = parts[0], parts[1], parts[2], parts[3]
    rq = rotary(heads(rq, RET_HEADS), pos)
    rk = rotary(heads(rk, RET_HEADS), pos) * (RET_DK ** -0.5)
    ro = retention(rq, rk, heads(rv, RET_HEADS))
    mu = jnp.mean(ro, axis=-1, keepdims=True)
    var = jnp.mean(jnp.square(ro - mu), axis=-1, keepdims=True)
    ro = ((ro - mu) * lax.rsqrt(var + GN_EPS)).transpose(0, 2, 1, 3).reshape(B, S, RET_V_W)
    ro = (ro * gn_g.astype(jnp.float32) + gn_b.astype(jnp.float32)).astype(h.dtype)
    ret_out = (jax.nn.silu(rg) * ro) @ w_ret_out

    outs, lses = [], []
    for gi, (win, dil) in enumerate(ATT_GROUPS):
        aq, ak, av = parts[4 + 3 * gi], parts[5 + 3 * gi], parts[6 + 3 * gi]
        tab = rel_bias[:, gi * ATT_HEADS_PER_GROUP:(gi + 1) * ATT_HEADS_PER_GROUP]
        o, lse = dilated_group(heads(aq, ATT_HEADS_PER_GROUP), heads(ak, ATT_HEADS_PER_GROUP),
                               heads(av, ATT_HEADS_PER_GROUP), tab, win, dil)
        outs.append(o)
        lses.append(lse)
    o_all = jnp.stack(outs, axis=0)
    wts = jax.nn.softmax(jnp.stack(lses, axis=0), axis=0)
    att = jnp.einsum('gbhs,gbhsd->bshd', wts.astype(o_all.dtype), o_all).reshape(B, S, ATT_GROUP_W)
    att_out = att @ w_att_out

    gate_a, gate_b = parts[-2], parts[-1]
    merged = jax.nn.sigmoid(gate_a) * ret_out + jax.nn.sigmoid(gate_b) * att_out
    return merged @ w_o


def squared_relu_mlp(h, w1, w2):
    return jnp.square(jax.nn.relu(h @ w1)) @ w2


def setup_inputs(seed: int = 0) -> dict:
    key = jax.random.key(seed)
    ks = jax.random.split(key, 18)
    nrm = jax.random.normal
    f32 = jnp.float32
    return {
        "x": nrm(ks[0], (BATCH, SEQ, D_MODEL), f32),
        "c": nrm(ks[1], (BATCH, D_MODEL), f32),
        "w_ada": nrm(ks[2], (DEPTH, D_MODEL, 6 * D_MODEL), f32) * D_MODEL ** -0.5,
        "b_ada": nrm(ks[3], (DEPTH, 6 * D_MODEL), f32) * 0.02,
        "norm1_g": 1.0 + 0.02 * nrm(ks[4], (DEPTH, D_MODEL), f32),
        "w_in": nrm(ks[5], (DEPTH, D_MODEL, IN_COLS), f32) * D_MODEL ** -0.5,
        "rel_bias": nrm(ks[6], (REL_BUCKETS, N_ATT_HEADS), f32) * 0.5,
        "ret_gn_g": 1.0 + 0.02 * nrm(ks[7], (DEPTH, RET_V_W), f32),
        "ret_gn_b": 0.02 * nrm(ks[8], (DEPTH, RET_V_W), f32),
        "w_ret_out": nrm(ks[9], (DEPTH, RET_V_W, D_MODEL), f32) * RET_V_W ** -0.5,
        "w_att_out": nrm(ks[10], (DEPTH, ATT_GROUP_W, D_MODEL), f32) * ATT_GROUP_W ** -0.5,
        "w_o": nrm(ks[11], (DEPTH, D_MODEL, D_MODEL), f32) * D_MODEL ** -0.5,
        "norm2_g": 1.0 + 0.02 * nrm(ks[12], (DEPTH, D_MODEL), f32),
        "w_ff1": nrm(ks[13], (DEPTH, D_MODEL, D_FF), f32) * D_MODEL ** -0.5,
        "w_ff2": nrm(ks[14], (DEPTH, D_FF, D_MODEL), f32) * D_FF ** -0.5,
        "norm_f_g": 1.0 + 0.02 * nrm(ks[15], (D_MODEL,), f32),
    }


def reference(x, c, w_ada, b_ada, norm1_g, w_in, rel_bias, ret_gn_g, ret_gn_b,
              w_ret_out, w_att_out, w_o, norm2_g, w_ff1, w_ff2, norm_f_g):
    for l in range(DEPTH):
        mod = jax.nn.silu(c) @ w_ada[l] + b_ada[l]
        sh1, sc1, g1, sh2, sc2, g2 = jnp.split(mod, 6, axis=-1)
        h = modulate(rms_norm(x, norm1_g[l]), sh1, sc1)
        x = x + g1[:, None, :] * token_mixer(h, w_in[l], rel_bias, ret_gn_g[l], ret_gn_b[l],
                                             w_ret_out[l], w_att_out[l], w_o[l])
        h = modulate(rms_norm(x, norm2_g[l]), sh2, sc2)
        x = x + g2[:, None, :] * squared_relu_mlp(h, w_ff1[l], w_ff2[l])
    return rms_norm(x, norm_f_g)
```

```python
import math
from contextlib import ExitStack

import numpy as np
import concourse.bass as bass
import concourse.mybir as mybir
from concourse.bass_utils import run_bass_kernel_spmd

F32 = mybir.dt.float32
BF16 = mybir.dt.bfloat16
AF = mybir.ActivationFunctionType
ALU = mybir.AluOpType

S = 2048
D = 1024
NB = 8
IN_COLS = 12800
GROUPS = ((128, 1), (512, 4), (2048, 16))

_ESZ = {str(F32): 4, str(BF16): 2, str(mybir.dt.int32): 4, str(mybir.dt.uint32): 4}
PSUM_BANK = 2048


def _region(ap):
    name = ap.tensor.name
    dims = ap.ap
    esz = _ESZ[str(ap.dtype)]
    off = int(ap.offset)
    space = str(ap.space).upper()
    if "DRAM" in space:
        ext = sum((c - 1) * abs(s) for s, c in dims) + 1
        return (name, 0, 1, off * esz, (off + ext) * esz)
    pstep, npart = dims[0]
    p0 = off // pstep if pstep else 0
    f0 = off % pstep if pstep else off
    ext = sum((c - 1) * abs(s) for s, c in dims[1:]) + 1
    blo, bhi = f0 * esz, (f0 + ext) * esz
    if "PSUM" in space:
        blo = (blo // PSUM_BANK) * PSUM_BANK
        bhi = -(-bhi // PSUM_BANK) * PSUM_BANK
    return (name, p0, p0 + npart, blo, bhi)


def _ovl(a, b):
    return a[1] < b[2] and b[1] < a[2] and a[3] < b[4] and b[3] < a[4]


def _covers(a, b):
    return a[1] <= b[1] and a[2] >= b[2] and a[3] <= b[3] and a[4] >= b[4]


class _Op:
    __slots__ = ("id", "eng", "fn", "dma", "deps", "signal", "token")


class Prog:
    ENGS = ("pe", "act", "dve", "pool", "sp")

    def __init__(self, nc, n_dma_sems=40):
        self.nc = nc
        self.ops = []
        self.eng_ops = {e: [] for e in self.ENGS}
        self.wrecs = {}
        self.rrecs = {}
        self.n_dma_sems = n_dma_sems
        self.out_dma_ops = []

    def op(self, eng, fn, reads=(), writes=(), dma=False, is_out=False, after=()):
        o = _Op()
        o.id = len(self.ops)
        o.eng = eng
        o.fn = fn
        o.dma = dma
        o.signal = False
        o.token = None
        deps = set()
        rregs = [_region(a) for a in reads]
        wregs = [_region(a) for a in writes]
        for r in rregs:
            for w in self.wrecs.get(r[0], ()):
                if _ovl(r, w[0]):
                    deps.add(w[1])
        for wr in wregs:
            for w in self.wrecs.get(wr[0], ()):
                if _ovl(wr, w[0]):
                    deps.add(w[1])
            for key, oid in self.rrecs.get(wr[0], {}).items():
                if _ovl(wr, key[1]):
                    deps.add(oid)
        for r in rregs:
            key = (("dma", o.id) if dma else eng, r)
            self.rrecs.setdefault(r[0], {})[key] = o.id
        for wr in wregs:
            lst = self.wrecs.setdefault(wr[0], [])
            lst[:] = [w for w in lst if not _covers(wr, w[0])]
            lst.append((wr, o.id))
            rd = self.rrecs.get(wr[0])
            if rd:
                for key in [k for k in rd if _covers(wr, k[1])]:
                    del rd[key]
        deps.update(after)
        deps.discard(o.id)
        best = {}
        red = set()
        for d in deps:
            dop = self.ops[d]
            if dop.dma:
                red.add(d)
            else:
                if dop.eng == eng and eng == "pe" and not dma:
                    continue
                if dop.eng not in best or best[dop.eng] < d:
                    best[dop.eng] = d
        red.update(best.values())
        o.deps = red
        self.ops.append(o)
        self.eng_ops[eng].append(o)
        if is_out:
            self.out_dma_ops.append(o.id)
        return o.id

    def dma(self, eng, out, in_, is_out=False, track_in=True, after=()):
        rd = [in_] if track_in else []
        return self.op(eng, lambda e, out=out, in_=in_: e.dma_start(out=out, in_=in_),
                       reads=rd, writes=[out], dma=True, is_out=is_out, after=after)

    def finalize(self, es):
        nc = self.nc
        fin = _Op()
        fin.id = len(self.ops)
        fin.eng = "sp"
        fin.fn = None
        fin.dma = False
        fin.signal = False
        fin.token = None
        fin.deps = set(self.out_dma_ops)
        self.ops.append(fin)
        self.eng_ops["sp"].append(fin)
        dma_ops = [o for o in self.ops if o.dma]
        N = self.n_dma_sems
        for j, o in enumerate(dma_ops):
            if j >= N:
                o.deps.add(dma_ops[j - N].id)
        for o in self.ops:
            for d in o.deps:
                self.ops[d].signal = True
        for o in dma_ops:
            o.signal = True
        self.eng_sem = {e: es.enter_context(nc.semaphore("s_" + e)) for e in self.ENGS}
        self.dma_sems = [es.enter_context(nc.semaphore("d%d" % i)) for i in range(N)]
        cnt = {e: 0 for e in self.ENGS}
        for j, o in enumerate(dma_ops):
            o.token = (self.dma_sems[j % N], 16 * (j // N + 1))
        for e in self.ENGS:
            for o in self.eng_ops[e]:
                if o.dma:
                    continue
                if o.signal:
                    cnt[e] += 1
                    o.token = (self.eng_sem[e], cnt[e])

    def emit(self, ename, e):
        known = {}
        for o in self.eng_ops[ename]:
            need = {}
            for d in o.deps:
                sem, val = self.ops[d].token
                k = id(sem)
                if k not in need or need[k][1] < val:
                    need[k] = (sem, val)
            for k, (sem, val) in need.items():
                if known.get(k, 0) < val:
                    e.wait_ge(sem, val)
                    known[k] = val
            if o.fn is None:
                continue
            ins = o.fn(e)
            if o.signal:
                ins.then_inc(o.token[0], 16 if o.dma else 1)

    def run_block(self):
        nc = self.nc
        with nc.Block() as block:
            @block.tensor
            def _(e):
                self.emit("pe", e)

            @block.scalar
            def _(e):
                self.emit("act", e)

            @block.vector
            def _(e):
                self.emit("dve", e)

            @block.gpsimd
            def _(e):
                self.emit("pool", e)

            @block.sync
            def _(e):
                self.emit("sp", e)


def _isap(x):
    return not isinstance(x, (int, float)) and x is not None


class _Stop(Exception):
    pass


def build_program(debug=False, stop=None, skip=()):
    nc = bass.Bass("TRN2", target_bir_lowering=False)
    dt_in = lambda name, shape, dt=F32: nc.dram_tensor(name, list(shape), dt, kind="ExternalInput").ap()
    xT_d = dt_in("xT", [D, S])
    ccol_d = dt_in("ccol", [128, 8])
    vecs_d = dt_in("vecs", [128, 104])
    w_ada_d = dt_in("w_ada", [D, 6 * D])
    w_in_d = dt_in("w_in", [D, IN_COLS])
    w_ro_d = dt_in("w_ret_out", [2048, D])
    w_ao_d = dt_in("w_att_out", [512, D])
    w_o_d = dt_in("w_o", [D, D])
    w_f1_d = dt_in("w_ff1", [D, 4 * D])
    w_f2_d = dt_in("w_ff2", [4 * D, D])
    biasg_d = dt_in("biasg", [128, 12, 256])
    cs_d = dt_in("cossin", [128, 2, S])
    rconst_d = dt_in("rconst", [128, 4 * 128 + 4 * 128 + 4 + 256 + 128])
    outT_d = nc.dram_tensor("outT", [D, S], F32, kind="ExternalOutput").ap()
    retT_d = nc.dram_tensor("retT_scr", [2048, S], BF16, kind="ExternalOutput").ap()
    dbg = {}
    if debug:
        dbg["hT"] = nc.dram_tensor("dbg_hT", [128, 8, S], BF16, kind="ExternalOutput").ap()
        dbg["attT"] = nc.dram_tensor("dbg_attT", [128, 4, S], BF16, kind="ExternalOutput").ap()
        dbg["retT"] = nc.dram_tensor("dbg_retT", [128, 16, S], BF16, kind="ExternalOutput").ap()
        dbg["x1T"] = nc.dram_tensor("dbg_x1T", [128, 8, S], F32, kind="ExternalOutput").ap()
        dbg["modT"] = nc.dram_tensor("dbg_modT", [128, 48], F32, kind="ExternalOutput").ap()

    with ExitStack() as es:
        P = Prog(nc)
        sb = lambda name, shape, dt: es.enter_context(nc.sbuf_tensor("sb_" + name, list(shape), dt))
        vecs = sb("vecs", [128, 104], F32)
        modT = sb("modT", [128, 48], F32)
        cols = sb("cols", [128, 64], F32)
        rconst = sb("rconst", [128, 4 * 128 + 4 * 128 + 4 + 256 + 128], F32)
        cbf = sb("cbf", [128, 4 * 128 + 128 + 128 + 8], BF16)
        hT = sb("hT", [128, 8, S], BF16)
        attT = sb("attT", [128, 4, S], BF16)
        NST, NRING = 2, 3
        wstage = [sb("wst%d" % i, [128, 2048], F32) for i in range(NST)]
        wring = [sb("wrg%d" % i, [128, 4096], BF16) for i in range(NRING)]
        ARENA_B = 104 * 1024
        arena = sb("arena", [128, ARENA_B // 2], BF16)
        psum = es.enter_context(nc.psum_tensor("ps", [128, 4096], F32))

        def bank(i):
            return psum[:, i * 512:(i + 1) * 512]

        class Arena:
            def __init__(self):
                self.off = 0

            def reset(self, off=0):
                self.off = off

            def get(self, n, dt):
                esz = 4 if dt == F32 else 2
                self.off = -(-self.off // 64) * 64
                o = self.off
                self.off += n * esz
                assert self.off <= ARENA_B, ("arena overflow", self.off)
                v = arena[:, o // 2:(o + n * esz) // 2]
                return v.bitcast(F32) if dt == F32 else v

        A = Arena()

        def mm(out, lhsT, rhs, start, stop, skip=False):
            P.op("pe", lambda e: e.matmul(out, lhsT=lhsT, rhs=rhs, start=start, stop=stop, skip_group_check=skip),
                 reads=[lhsT, rhs], writes=[out])

        def tr(out, in_):
            P.op("pe", lambda e: e.transpose(out=out, in_=in_, identity=ident_bf), reads=[in_, ident_bf], writes=[out])

        def act(out, in_, func, scale=1.0, bias=None):
            rd = [in_] + [a for a in (scale, bias) if _isap(a)]
            kw = dict(out=out, in_=in_, func=func, scale=scale)
            if bias is not None:
                kw["bias"] = bias
            P.op("act", lambda e: e.activation(**kw), reads=rd, writes=[out])

        def tt(eng, out, in0, in1, op):
            P.op(eng, lambda e: e.tensor_tensor(out=out, in0=in0, in1=in1, op=op), reads=[in0, in1], writes=[out])

        def ts(eng, out, in0, s1, s2, op0, op1=None):
            rd = [in0] + [a for a in (s1, s2) if _isap(a)]
            if op1 is None:
                P.op(eng, lambda e: e.tensor_scalar(out=out, in0=in0, scalar1=s1, scalar2=None, op0=op0), reads=rd, writes=[out])
            else:
                P.op(eng, lambda e: e.tensor_scalar(out=out, in0=in0, scalar1=s1, scalar2=s2, op0=op0, op1=op1), reads=rd, writes=[out])

        def stt(out, in0, scalar, in1, op0, op1):
            rd = [in0, in1] + ([scalar] if _isap(scalar) else [])
            P.op("dve", lambda e: e.scalar_tensor_tensor(out=out, in0=in0, scalar=scalar, in1=in1, op0=op0, op1=op1),
                 reads=rd, writes=[out])

        def cp(eng, out, in_):
            if eng == "act":
                act(out, in_, AF.Copy)
            else:
                P.op(eng, lambda e: e.tensor_copy(out=out, in_=in_), reads=[in_], writes=[out])

        def recip(out, in_):
            P.op("dve", lambda e: e.reciprocal(out=out, in_=in_), reads=[in_], writes=[out])

        evac_rr = [0]

        def evac(out, in_):
            evac_rr[0] ^= 1
            cp("act" if evac_rr[0] else "dve", out, in_)

        wstate = {"s": 0, "r": 0, "o": 0}

        def wload(W, r0, R, c0, C, share=False, engs=("pool",)):
            if share:
                so = wstate["o"]
                assert so + R * C <= 4096
                slot = wring[(wstate["r"] - 1) % len(wring)]
            else:
                assert R * C <= 4096
                so = 0
                slot = wring[wstate["r"] % len(wring)]
                wstate["r"] += 1
            wstate["o"] = so + R * C
            view = slot[:, so:so + R * C].rearrange("p (r c) -> p r c", r=R)
            rpp = max(1, 2048 // C)
            for r in range(0, R, rpp):
                rr = min(rpp, R - r)
                st = wstage[wstate["s"] % NST]
                wstate["s"] += 1
                stv = st[:, 0:rr * C].rearrange("p (r c) -> p r c", r=rr)
                src = W[(r0 + r) * 128:(r0 + r + rr) * 128, c0:c0 + C].rearrange("(r p) c -> p r c", p=128)
                P.dma("sp", stv, src, track_in=False)
                cp(engs[wstate["s"] % len(engs)], view[:, r:r + rr, :], stv)
            return view

        try:
            P.dma("sp", vecs[:], vecs_d, track_in=False)
            P.dma("sp", rconst[:], rconst_d, track_in=False)
            OFF_DEC, OFF_QD, OFF_KD, OFF_MASK, OFF_ID = 0, 512, 1024, 1028, 1284
            decay_bf = cbf[:, 0:512]
            ident_bf = cbf[:, 512:640]
            ones_bf = cbf[:, 640:768]
            cp("dve", decay_bf, rconst[:, OFF_DEC:OFF_DEC + 512])
            cp("dve", ident_bf, rconst[:, OFF_ID:OFF_ID + 128])
            P.op("dve", lambda e: e.memset(ones_bf, 1.0), writes=[ones_bf])
            qd_f = rconst[:, OFF_QD:OFF_QD + 512]
            kdec = rconst[:, OFF_KD:OFF_KD + 4]
            mask256 = rconst[:, OFF_MASK:OFF_MASK + 256]
            a1 = cols[:, 0:8]
            a2 = cols[:, 8:16]
            g1h = cols[:, 16:24]
            P.op("dve", lambda e: e.memset(cols[:, 24:25], 1e-6), writes=[cols[:, 24:25]])
            P.op("dve", lambda e: e.memset(cols[:, 25:26], 256.0 * 1e-5), writes=[cols[:, 25:26]])
            P.op("dve", lambda e: e.memset(cols[:, 26:27], -0.5), writes=[cols[:, 26:27]])
            eps_rms = cols[:, 24:25]
            eps_gn = cols[:, 25:26]
            mhalf = cols[:, 26:27]
            n1g, n2g, nfg = vecs[:, 48:56], vecs[:, 56:64], vecs[:, 64:72]

            A.reset()
            cc = A.get(8, F32)
            ct = A.get(8, F32)
            sbf = cbf[:, 768:776]
            P.dma("sp", cc, ccol_d, track_in=False)
            act(ct, cc, AF.Tanh, scale=0.5)
            ts("dve", ct, ct, 0.5, 0.5, ALU.mult, ALU.add)
            tt("dve", sbf, ct, cc, ALU.mult)
            sh1, sc1, g1c, sh2, sc2, g2c = [modT[:, i * 8:(i + 1) * 8] for i in range(6)]

            def mod_units(u0, u1, pmod):
                for u in range(u0, u1):
                    wv_ = wload(w_ada_d, 0, 8, u * 256, 256, engs=("act",))
                    for jj in range(2):
                        j = 2 * u + jj
                        for k in range(8):
                            mm(pmod[:, j:j + 1], wv_[:, k, jj * 128:(jj + 1) * 128], sbf[:, k:k + 1], k == 0, k == 7)
                tt("dve", modT[:, 2 * u0:2 * u1], pmod[:, 2 * u0:2 * u1], vecs[:, 2 * u0:2 * u1], ALU.add)

            def rms_stats(src_tile, pb):
                sq = A.get(8 * 512, BF16).rearrange("p (k t) -> p k t", k=8)
                act(sq, src_tile, AF.Square)
                for k in range(8):
                    mm(pb, ones_bf, sq[:, k, :], k == 0, k == 7)
                rs = A.get(512, F32)
                act(rs, pb, AF.Sqrt, scale=1.0 / D, bias=eps_rms)
                rstd = A.get(512, F32)
                recip(rstd, rs)
                return rstd

            mod_units(0, 8, bank(0)[:, 0:48])
            stt(a1, sc1, 1.0, n1g, ALU.add, ALU.mult)
            xts, rstds = [], []
            A.reset(1024)
            xids = []
            for tg in range(4):
                xt = A.get(8 * 512, F32).rearrange("p (k t) -> p k t", k=8)
                xids.append(P.dma("sp", xt, xT_d[:, tg * 512:(tg + 1) * 512].rearrange("(k p) t -> p k t", p=128),
                                  track_in=False, after=xids[-2:-1] if len(xids) >= 2 else ()))
                xts.append(xt)
            n1base = A.off
            for tg in range(4):
                A.reset(n1base + (tg % 2) * 16 * 1024)
                rstd = rms_stats(xts[tg], bank(1 + tg % 2))
                tmp = [A.get(512, F32) for _ in range(2)]
                for k in range(8):
                    t = tmp[k % 2]
                    tt("dve", t, xts[tg][:, k, :], rstd, ALU.mult)
                    act(hT[:, k, tg * 512:(tg + 1) * 512], t, AF.Identity, scale=a1[:, k:k + 1], bias=sh1[:, k:k + 1])
            if debug:
                P.dma("sp", dbg["hT"], hT[:], is_out=True)
            if stop == "A2":
                raise _Stop()

            A.reset()
            wv_units = [wload(w_in_d, 0, 8, 6144 + g_ * 1536 + 1024, 512) for g_ in range(3)]
            expb = A.get(12 * 256, BF16).rearrange("p (g c) -> p g c", g=12)
            v_all = [A.get(16 * 512, BF16).rearrange("p (b c) -> p b c", b=16) for _ in range(3)]
            qTg = [A.get(S, BF16) for _ in range(3)]
            kTg = [A.get(S, BF16) for _ in range(3)]
            au_off = -(-A.off // 64) * 64
            attU = A.get(S, F32)
            denU = A.get(S, F32)
            ebuf = [A.get(256, BF16) for _ in range(3)]
            pbuf = [A.get(256, BF16) for _ in range(3)]
            bg = arena[:, au_off // 2:au_off // 2 + 2 * 12 * 256].bitcast(F32).rearrange("p (g c) -> p g c", g=12)
            den_tmp = arena[:, au_off // 2 + 2 * 12 * 256:au_off // 2 + 2 * 12 * 256 + 512].bitcast(F32)
            P.dma("sp", bg, biasg_d, track_in=False)
            for gh in range(12):
                tt("dve", bg[:, gh, :], bg[:, gh, :], mask256, ALU.mult)
            ts("dve", den_tmp, mask256, 30000.0, -30000.0, ALU.mult, ALU.add)
            for gh in range(12):
                tt("dve", expb[:, gh, :], bg[:, gh, :], den_tmp, ALU.add)
            pj = [0]

            def pbank():
                pj[0] ^= 1
                return bank(pj[0])

            for g, (win, dil) in enumerate(GROUPS):
                base = 6144 + g * 1536
                wv = wv_units[g]
                nsub = 16 // dil
                for bi in range(16):
                    r, n = bi // nsub, bi % nsub
                    t0 = r + dil * 128 * n
                    pb = pbank()
                    for k in range(8):
                        mm(pb, hT[:, k, t0:t0 + 127 * dil + 1:dil], wv[:, k, :], k == 0, k == 7)
                    evac(v_all[g][:, bi, :], pb)
            mod_next = [8]

            def mod_some(n):
                for _ in range(n):
                    u = mod_next[0]
                    if u >= 24:
                        return
                    mod_next[0] += 1
                    pb_ = pbank()
                    wv_ = wload(w_ada_d, 0, 8, u * 256, 256, engs=("act", "dve"))
                    for jj in range(2):
                        for k in range(8):
                            mm(pb_[:, jj:jj + 1], wv_[:, k, jj * 128:(jj + 1) * 128], sbf[:, k:k + 1], k == 0, k == 7)
                    tt("dve", modT[:, 2 * u:2 * u + 2], pb_[:, 0:2], vecs[:, 2 * u:2 * u + 2], ALU.add)
            if stop == "B0":
                raise _Stop()
            scale_q = 1.0 / math.sqrt(128.0)
            tile_ctr = [0]
            for h in range(0 if "B" in skip else 4):
                for g, (win, dil) in enumerate(GROUPS):
                    base = 6144 + g * 1536
                    L = S // dil
                    nsub = 16 // dil
                    wq = wload(w_in_d, 0, 8, base + h * 128, 128)
                    wk = wload(w_in_d, 0, 8, base + 512 + h * 128, 128, share=True)
                    for (wt, dst, scl) in ((wq, qTg[g], scale_q), (wk, kTg[g], 1.0)):
                        for tg in range(4):
                            pb = pbank()
                            for k in range(8):
                                mm(pb, wt[:, k, :], hT[:, k, tg * 512:(tg + 1) * 512], k == 0, k == 7)
                            if dil == 1:
                                act(dst[:, tg * 512:(tg + 1) * 512], pb, AF.Copy, scale=scl)
                            else:
                                w_ = 512 // dil
                                src_v = pb.rearrange("p (i r) -> p r i", r=dil)
                                dst_v = dst.rearrange("p (r l) -> p r l", r=dil)[:, :, tg * w_:(tg + 1) * w_]
                                act(dst_v, src_v, AF.Copy, scale=scl)
                    mod_some(2 if (h * 3 + g) < 4 else 1)
                    alltiles = []
                    for R in range(4):
                        tiles = []
                        if dil == 1:
                            for n in range(max(0, 4 * R - 1), 4 * R + 4):
                                qs = [q for q in (n, n + 1) if 4 * R <= q <= 4 * R + 3]
                                tiles.append((n, n, qs[0], len(qs)))
                        elif dil == 4:
                            for n in range(4):
                                qs = [q for q in (n, n + 1) if q <= 3]
                                tiles.append((R * 4 + n, n, R * 4 + qs[0], len(qs)))
                        else:
                            for r in range(4 * R, 4 * R + 4):
                                tiles.append((r, 0, r, 1))
                        for ti, t_ in enumerate(tiles):
                            alltiles.append((R, ti == 0, ti == len(tiles) - 1) + t_)

                    def stage1(t_):
                        R, first, last, kb, kn, qlo, nq = t_
                        W_ = nq * 128
                        sb_ = bank(2 + tile_ctr[0] % 2)[:, 0:W_]
                        pm = pbuf[tile_ctr[0] % 3][:, 0:W_]
                        tile_ctr[0] += 1
                        qn_lo = qlo - (kb - kn)
                        moff = (qn_lo - kn) * 128
                        mm(sb_, kTg[g][:, kb * 128:(kb + 1) * 128], qTg[g][:, qlo * 128:qlo * 128 + W_], True, False)
                        mm(sb_, ident_bf, expb[:, g * 4 + h, moff:moff + W_], False, True)
                        act(pm, sb_, AF.Exp)
                        return pm

                    def stage2(t_, pm):
                        R, first, last, kb, kn, qlo, nq = t_
                        W_ = nq * 128
                        oacc = bank(4 + 2 * (R % 2))
                        dacc = bank(5 + 2 * (R % 2))
                        c0 = (qlo - 4 * R) * 128
                        mm(oacc[:, c0:c0 + W_], v_all[g][:, kb, h * 128:(h + 1) * 128], pm, first, last, skip=True)
                        mm(dacc[:, c0:c0 + W_], ones_bf, pm, first, last, skip=True)
                        if not last:
                            return
                        if dil == 1:
                            cp("dve", attU[:, R * 512:(R + 1) * 512], oacc)
                            cp("dve", denU[:, R * 512:(R + 1) * 512], dacc)
                        elif dil == 4:
                            va = attU[:, R:S:4]
                            vd = denU[:, R:S:4]
                            tt("dve", va, oacc, va, ALU.add)
                            tt("dve", vd, dacc, vd, ALU.add)
                        else:
                            va = attU.rearrange("p (i r) -> p r i", r=16)[:, 4 * R:4 * R + 4, :]
                            vd = denU.rearrange("p (i r) -> p r i", r=16)[:, 4 * R:4 * R + 4, :]
                            po = oacc.rearrange("p (a i) -> p a i", a=4)
                            pd = dacc.rearrange("p (a i) -> p a i", a=4)
                            tt("dve", va, po, va, ALU.add)
                            tt("dve", vd, pd, vd, ALU.add)

                    prev = None
                    for t_ in alltiles:
                        pm = stage1(t_)
                        if prev is not None:
                            stage2(*prev)
                        prev = (t_, pm)
                    stage2(*prev)
                recip(denU, denU)
                tt("dve", attT[:, h, :], attU, denU, ALU.mult)
            mod_some(24)
            stt(a2, sc2, 1.0, n2g, ALU.add, ALU.mult)
            ts("dve", g1h, g1c, 0.5, None, ALU.mult)
            if debug:
                P.dma("sp", dbg["modT"], modT[:], is_out=True)
            if debug:
                P.dma("sp", dbg["attT"], attT[:], is_out=True)
            if stop == "B":
                raise _Stop()

            A.reset()
            cst = [A.get(2 * 512, F32).rearrange("p (a t) -> p a t", a=2) for _ in range(2)]
            qT = A.get(2 * S, BF16).rearrange("p (c t) -> p c t", c=2)
            qTd = A.get(2 * S, BF16).rearrange("p (c t) -> p c t", c=2)
            kT = A.get(2 * S, BF16).rearrange("p (c t) -> p c t", c=2)
            ktok = A.get(16 * 256, BF16).rearrange("p (b c) -> p b c", b=16)
            vtok = A.get(4 * 512, BF16).rearrange("p (b c) -> p b c", b=4)
            sgT = A.get(4 * 1024, BF16).rearrange("p (f t) -> p f t", f=4)
            Sst = A.get(2 * 512, F32).rearrange("p (c e) -> p c e", c=2)
            Sbf = A.get(2 * 512, BF16).rearrange("p (c e) -> p c e", c=2)
            rt = [A.get(512, F32) for _ in range(4)]
            PTb = [A.get(128, BF16) for _ in range(2)]
            ybuf = [A.get(512, BF16) for _ in range(3)]
            stats = [A.get(16, F32) for _ in range(3)]
            affb = [A.get(512, BF16).rearrange("p (f t) -> p f t", f=4) for _ in range(2)]
            rstage = [A.get(4 * 512, BF16).rearrange("p (f t) -> p f t", f=4) for _ in range(2)]
            gnc = cols[:, 32:64]
            ts("dve", gnc, vecs[:, 72:104], 0.5, None, ALU.mult)
            csi = [0]
            rti = [0]
            pjc = [0]

            def pjbank():
                pjc[0] ^= 1
                return bank(1 if pjc[0] else 7)

            NH = 0 if "C" in skip else 4

            def load_qk(h_):
                wq_ = wload(w_in_d, 0, 8, h_ * 256, 256)
                wk_ = wload(w_in_d, 0, 8, 1024 + h_ * 256, 256, share=True)
                return wq_, wk_

            nxt_qk = load_qk(0) if NH else None
            for h in range(NH):
                wq, wk = nxt_qk
                for tg in range(4):
                    tsl = slice(tg * 512, (tg + 1) * 512)
                    cs = cst[csi[0] % 2]
                    csi[0] += 1
                    P.dma("sp", cs, cs_d[:, :, tsl], track_in=False)
                    cosv, sinv = cs[:, 0, :], cs[:, 1, :]
                    for (wt, dst) in ((wq, qT), (wk, kT)):
                        pr = rti[0] % 2
                        rti[0] += 1
                        p1, p2 = bank(2 * pr), bank(2 * pr + 1)
                        r0, r1, r2, r3 = rt[0:4]
                        for k in range(8):
                            mm(p1, wt[:, k, 0:128], hT[:, k, tsl], k == 0, k == 7)
                        for k in range(8):
                            mm(p2, wt[:, k, 128:256], hT[:, k, tsl], k == 0, k == 7)
                        tt("dve", r0, p1, cosv, ALU.mult)
                        tt("dve", r1, p2, sinv, ALU.mult)
                        tt("dve", r2, p1, sinv, ALU.mult)
                        tt("dve", r3, p2, cosv, ALU.mult)
                        tt("dve", dst[:, 0, tsl], r0, r1, ALU.subtract)
                        tt("dve", dst[:, 1, tsl], r2, r3, ALU.add)
                wv = wload(w_in_d, 0, 8, 2048 + h * 512, 512, engs=("pool", "act"))
                wg = wload(w_in_d, 0, 8, 4096 + h * 512, 512, engs=("pool", "act"))
                qdb = qd_f[:, h * 128:(h + 1) * 128]
                qdb16 = qdb.unsqueeze(1).broadcast_to([128, 16, 128])
                for c in range(2):
                    tt("pool", qTd[:, c, :].rearrange("p (b i) -> p b i", b=16),
                       qT[:, c, :].rearrange("p (b i) -> p b i", b=16), qdb16, ALU.mult)
                for q4 in range(4):
                    pbt = bank(4 + q4 % 2).bitcast(BF16)
                    for cj in range(4):
                        ci = q4 * 4 + cj
                        for c in range(2):
                            tr(pbt[:, cj * 256 + c * 128:cj * 256 + (c + 1) * 128], kT[:, c, ci * 128:(ci + 1) * 128])
                    act(ktok[:, q4 * 4:(q4 + 1) * 4, :], pbt.rearrange("p (b c) -> p b c", b=4), AF.Identity,
                        scale=kdec[:, h:h + 1])

                def vproj(ci):
                    pb = pjbank()
                    for k in range(8):
                        mm(pb, hT[:, k, ci * 128:(ci + 1) * 128], wv[:, k, :], k == 0, k == 7)
                    evac(vtok[:, ci % 4, :], pb)

                def gate_tile(fc, tg):
                    tsl = slice(tg * 512, (tg + 1) * 512)
                    pg = pjbank()
                    th = rt[(fc + tg) % 4]
                    for k in range(8):
                        mm(pg, wg[:, k, fc * 128:(fc + 1) * 128], hT[:, k, tsl], k == 0, k == 7)
                    act(th, pg, AF.Tanh, scale=0.5)
                    stt(sgT[:, fc, (tg % 2) * 512:(tg % 2 + 1) * 512], th, 1.0, pg, ALU.add, ALU.mult)

                vproj(0)
                for fc in range(4):
                    gate_tile(fc, 0)
                cdh = math.exp(math.log1p(-(2.0 ** (-5.0 - h))) * 128.0)
                gcol = gnc[:, h * 4:(h + 1) * 4]
                bcol = gnc[:, 16 + h * 4:16 + (h + 1) * 4]

                def st1(ci):
                    csl = slice(ci * 128, (ci + 1) * 128)
                    ps_s = bank(0)[:, 0:128]
                    for c in range(2):
                        mm(ps_s, kT[:, c, csl], qT[:, c, csl], c == 0, c == 1)
                    tt("dve", PTb[ci % 2], ps_s, decay_bf[:, h * 128:(h + 1) * 128], ALU.mult)

                def st2(ci):
                    csl = slice(ci * 128, (ci + 1) * 128)
                    po = bank(2 + ci % 2)
                    vt = vtok[:, ci % 4, :]
                    mm(po, PTb[ci % 2], vt, True, ci == 0)
                    if ci > 0:
                        for c in range(2):
                            mm(po, qTd[:, c, csl], Sbf[:, c, :], False, c == 1)
                    if ci < 15:
                        for c in range(2):
                            pu = bank(4 + c)
                            mm(pu, ktok[:, ci, c * 128:(c + 1) * 128], vt, True, True)
                            if ci == 0:
                                cp("dve", Sst[:, c, :], pu)
                            else:
                                stt(Sst[:, c, :], Sst[:, c, :], cdh, pu, ALU.mult, ALU.add)
                            act(Sbf[:, c, :], Sst[:, c, :], AF.Copy)
                    st = stats[ci % 3]
                    P.op("dve", lambda e, st=st, po=po: e.bn_stats(out=st[:, 0:6], in_=po), reads=[po], writes=[st[:, 0:6]])
                    P.op("dve", lambda e, st=st: e.bn_aggr(out=st[:, 6:8], in_=st[:, 0:6]), reads=[st[:, 0:6]], writes=[st[:, 6:8]])
                    ts("dve", st[:, 8:9], st[:, 7:8], eps_gn, None, ALU.add)
                    tt("pool", st[:, 9:10], st[:, 8:9], mhalf, ALU.pow)
                    ts("dve", st[:, 10:11], st[:, 6:7], st[:, 9:10], -1.0, ALU.mult, ALU.mult)
                    act(ybuf[ci % 3], po, AF.Identity, scale=st[:, 9:10], bias=st[:, 10:11])

                def st3(ci):
                    y = ybuf[ci % 3]
                    ptb = bank(6).bitcast(BF16)[:, 0:512].rearrange("p (f t) -> p f t", f=4)
                    for fc in range(4):
                        tr(ptb[:, fc, :], y[:, fc * 128:(fc + 1) * 128])
                    af = affb[ci % 2]
                    for fc in range(4):
                        act(af[:, fc, :], ptb[:, fc, :], AF.Identity, scale=gcol[:, fc:fc + 1], bias=bcol[:, fc:fc + 1])
                    rs_ = rstage[(ci // 4) % 2]
                    tgs = (ci // 4) % 2
                    tt("dve", rs_[:, :, (ci % 4) * 128:(ci % 4 + 1) * 128], af,
                       sgT[:, :, tgs * 512 + (ci % 4) * 128:tgs * 512 + (ci % 4 + 1) * 128], ALU.mult)
                    if ci % 4 == 3:
                        q4 = ci // 4
                        dst = retT_d[h * 512:(h + 1) * 512, q4 * 512:(q4 + 1) * 512].rearrange("(f p) t -> p f t", p=128)
                        P.dma("sp", dst, rs_)

                nxt_qk = load_qk(h + 1) if h + 1 < NH else None
                for i in range(18):
                    if i < 16:
                        st1(i)
                    if i + 1 < 16:
                        vproj(i + 1)
                    if 2 <= i < 14:
                        gate_tile((i - 2) % 4, (i - 2) // 4 + 1)
                    if 1 <= i <= 16:
                        st2(i - 1)
                    if i >= 2:
                        st3(i - 2)

            if stop == "C":
                raise _Stop()
            A.reset()
            mergedT = A.get(8 * S, BF16).rearrange("p (c t) -> p c t", c=8)
            big = A.off
            retT = A.get(16 * 1024, BF16).rearrange("p (k t) -> p k t", k=16)
            tmpm = [A.get(512, F32) for _ in range(4)]
            mi = [0]
            for th_ in range(2):
                for k in range(16):
                    P.dma("sp", retT[:, k, :], retT_d[k * 128:(k + 1) * 128, th_ * 1024:(th_ + 1) * 1024])
                if debug:
                    P.dma("sp", dbg["retT"][:, :, th_ * 1024:(th_ + 1) * 1024], retT, is_out=True)
                for c in range(8):
                    wro = wload(w_ro_d, 0, 16, c * 128, 128, engs=("pool", "act"))
                    wao = wload(w_ao_d, 0, 4, c * 128, 128, share=True)
                    wgab = wload(w_in_d, 0, 8, 10752 + c * 128, 128, engs=("pool", "act"))
                    wgb = wload(w_in_d, 0, 8, 11776 + c * 128, 128)
                    pAs = []
                    for tgl in range(2):
                        lsl = slice(tgl * 512, (tgl + 1) * 512)
                        pA = bank(tgl)
                        for k in range(16):
                            mm(pA, wro[:, k, :], retT[:, k, lsl], k == 0, k == 15)
                        pAs.append(pA)
                    for tgl in range(2):
                        tg = th_ * 2 + tgl
                        tsl = slice(tg * 512, (tg + 1) * 512)
                        pA = pAs[tgl]
                        pB, pC, pD = [bank(2 + tgl * 3 + i) for i in range(3)]
                        for k in range(4):
                            mm(pB, wao[:, k, :], attT[:, k, tsl], k == 0, k == 3)
                        for k in range(8):
                            mm(pC, wgab[:, k, :], hT[:, k, tsl], k == 0, k == 7)
                        for k in range(8):
                            mm(pD, wgb[:, k, :], hT[:, k, tsl], k == 0, k == 7)
                        sa, sb2 = tmpm[(mi[0] % 2) * 2], tmpm[(mi[0] % 2) * 2 + 1]
                        mi[0] += 1
                        act(sa, pC, AF.Tanh, scale=0.5)
                        act(sb2, pD, AF.Tanh, scale=0.5)
                        stt(sa, sa, 1.0, pA, ALU.add, ALU.mult)
                        stt(sb2, sb2, 1.0, pB, ALU.add, ALU.mult)
                        tt("dve", mergedT[:, c, tsl], sa, sb2, ALU.add)

            A.reset(big)
            x1T = A.get(8 * S, F32).rearrange("p (c t) -> p c t", c=8)
            xtmp = [A.get(512, F32) for _ in range(3)]
            xi = [0]
            wo_units = [wload(w_o_d, 0, 8, hf * 512, 512) for hf in range(2)]
            w1_first = wload(w_f1_d, 0, 8, 0, 512)
            fbase = A.off - 3 * 2048

            def phase_e(tg):
                tsl = slice(tg * 512, (tg + 1) * 512)
                for c2 in range(8):
                    wo = wo_units[c2 // 4][:, :, (c2 % 4) * 128:(c2 % 4 + 1) * 128]
                    pb = pbank()
                    for k in range(8):
                        mm(pb, wo[:, k, :], mergedT[:, k, tsl], k == 0, k == 7)
                    xt = xtmp[xi[0] % 3]
                    xi[0] += 1
                    P.dma("sp", xt, xT_d[c2 * 128:(c2 + 1) * 128, tsl], track_in=False)
                    stt(x1T[:, c2, tsl], pb, g1h[:, c2:c2 + 1], xt, ALU.mult, ALU.add)

            def phase_f(tg):
                tsl = slice(tg * 512, (tg + 1) * 512)
                rstd = rms_stats_small(x1T[:, :, tsl], bank(2 + tg % 2))
                for k in range(8):
                    t = ftmp[k % 4]
                    tt("dve", t, x1T[:, k, tsl], rstd, ALU.mult)
                    act(hT[:, k, tsl], t, AF.Identity, scale=a2[:, k:k + 1], bias=sh2[:, k:k + 1])

            def rms_stats_small(src_tile, pb):
                for k in range(8):
                    sq = fsq[k % 2]
                    act(sq, src_tile[:, k, :], AF.Square)
                    mm(pb, ones_bf, sq, k == 0, k == 7)
                act(frs, pb, AF.Sqrt, scale=1.0 / D, bias=eps_rms)
                recip(frs, frs)
                return frs

            fsq = [attT[:, 0, 0:512], attT[:, 0, 512:1024]]
            frs = attT[:, 0, 1024:2048].bitcast(F32)
            ftmp = [attT[:, 1 + i // 2, (i % 2) * 1024:(i % 2 + 1) * 1024].bitcast(F32) for i in range(4)]
            for tg in range(5):
                if tg < 4:
                    phase_e(tg)
                if tg >= 1:
                    phase_f(tg - 1)
            if debug:
                P.dma("sp", dbg["x1T"], x1T, is_out=True)
            if stop == "E":
                raise _Stop()

            A.reset(0)
            uT = A.get(8 * S, BF16).rearrange("p (f t) -> p f t", f=8)
            assert A.off <= big
            A.reset(fbase)
            rl = [A.get(512, BF16) for _ in range(3)]
            ri = [0]
            for fg in range(4):
                for half in range(2):
                    w1 = w1_first if (fg == 0 and half == 0) else wload(w_f1_d, 0, 8, fg * 1024 + half * 512, 512)
                    for tg in range(4):
                        tsl = slice(tg * 512, (tg + 1) * 512)
                        for fj in range(4):
                            fi = half * 4 + fj
                            pb = pbank()
                            for k in range(8):
                                mm(pb, w1[:, k, fj * 128:(fj + 1) * 128], hT[:, k, tsl], k == 0, k == 7)
                            r_ = rl[ri[0] % 3]
                            ri[0] += 1
                            ts("dve", r_, pb, 0.0, None, ALU.max)
                            act(uT[:, fi, tsl], r_, AF.Square)
                if fg < 3:
                    for half in range(2):
                        w2 = wload(w_f2_d, fg * 8, 8, half * 512, 512)
                        for cj in range(4):
                            c = half * 4 + cj
                            for tg in range(4):
                                tsl = slice(tg * 512, (tg + 1) * 512)
                                pb = bank(2 + (cj * 4 + tg) % 2)
                                for k in range(8):
                                    mm(pb, w2[:, k, cj * 128:(cj + 1) * 128], uT[:, k, tsl], k == 0, k == 7)
                                stt(x1T[:, c, tsl], pb, g2c[:, c:c + 1], x1T[:, c, tsl], ALU.mult, ALU.add)
                else:
                    w2u = [wload(w_f2_d, fg * 8, 8, half * 512, 512) for half in range(2)]
                    fo = [attT[:, 3, 0:1024].bitcast(F32), attT[:, 3, 1024:2048].bitcast(F32)]

                    def final_norm(tg):
                        tsl = slice(tg * 512, (tg + 1) * 512)
                        rstd = rms_stats_small(x1T[:, :, tsl], bank(4 + tg % 2))
                        for k in range(8):
                            ost = fo[k % 2]
                            stt(ost, x1T[:, k, tsl], nfg[:, k:k + 1], rstd, ALU.mult, ALU.mult)
                            P.dma("sp", outT_d[k * 128:(k + 1) * 128, tsl], ost, is_out=True)

                    for tg in range(5):
                        if tg < 4:
                            tsl = slice(tg * 512, (tg + 1) * 512)
                            for c in range(8):
                                w2 = w2u[c // 4]
                                cj = c % 4
                                pb = bank(2 + c % 2)
                                for k in range(8):
                                    mm(pb, w2[:, k, cj * 128:(cj + 1) * 128], uT[:, k, tsl], k == 0, k == 7)
                                stt(x1T[:, c, tsl], pb, g2c[:, c:c + 1], x1T[:, c, tsl], ALU.mult, ALU.add)
                        if tg >= 1:
                            final_norm(tg - 1)

        except _Stop:
            pass
        P.finalize(es)
        P.run_block()
    return nc


def _t5_bucket_np(dist):
    max_exact = 16
    d_f = np.maximum(dist, 1).astype(np.float32)
    large = max_exact + (np.log(d_f / np.float32(max_exact)) / np.float32(math.log(2048 / max_exact))
                         * np.float32(32 - max_exact)).astype(np.int32)
    large = np.minimum(large, 31)
    return np.where(dist < max_exact, dist, large)


def _col_layout(v):
    v = np.asarray(v, np.float32).reshape(-1, 128)
    return np.ascontiguousarray(v.T)


def _constants():
    i = np.arange(128, dtype=np.float64)
    rc = np.zeros((128, 4 * 128 + 4 * 128 + 4 + 256 + 128), np.float32)
    for h in range(4):
        lg = math.log1p(-(2.0 ** (-5.0 - h)))
        rel = i[None, :] - i[:, None]
        rc[:, h * 128:(h + 1) * 128] = np.where(rel >= 0, np.exp(lg * np.maximum(rel, 0.0)), 0.0)
        rc[:, 512 + h * 128:512 + (h + 1) * 128] = np.exp(lg * (i + 1.0))[None, :]
        rc[:, 1024 + h] = np.exp(lg * (127.0 - i))
    kj = np.arange(128)[:, None]
    qq = np.arange(256)[None, :]
    m = qq - kj
    rc[:, 1028:1284] = ((m >= 0) & (m <= 128)).astype(np.float32)
    rc[:, 1284:1412] = np.eye(128, dtype=np.float32)
    inv = 10000.0 ** (-np.arange(128, dtype=np.float64) / 128.0)
    ang = inv[:, None] * np.arange(S, dtype=np.float64)[None, :]
    cs = np.stack([np.cos(ang), np.sin(ang)], axis=1).astype(np.float32)
    return rc, np.ascontiguousarray(cs)


_CACHE = {}


def kernel(**inputs):
    x = np.asarray(inputs["x"], np.float32)
    c = np.asarray(inputs["c"], np.float32)
    B = x.shape[0]
    rc, cs = _constants()
    rel_bias = np.asarray(inputs["rel_bias"], np.float32)
    kj = np.arange(128)[:, None]
    qq = np.arange(256)[None, :]
    m = np.clip(qq - kj, 0, 128)
    biasg = np.zeros((128, 12, 256), np.float32)
    for g, (win, dil) in enumerate(GROUPS):
        bidx = _t5_bucket_np(m * dil)
        for h in range(4):
            biasg[:, g * 4 + h, :] = rel_bias[bidx, g * 4 + h]
    vecs = np.concatenate([_col_layout(inputs["b_ada"][0]), _col_layout(inputs["norm1_g"][0]),
                           _col_layout(inputs["norm2_g"][0]), _col_layout(inputs["norm_f_g"]),
                           _col_layout(inputs["ret_gn_g"][0]), _col_layout(inputs["ret_gn_b"][0])], axis=1)
    shared = {
        "vecs": np.ascontiguousarray(vecs, np.float32),
        "w_ada": np.ascontiguousarray(inputs["w_ada"][0], np.float32),
        "w_in": np.ascontiguousarray(inputs["w_in"][0], np.float32),
        "w_ret_out": np.ascontiguousarray(inputs["w_ret_out"][0], np.float32),
        "w_att_out": np.ascontiguousarray(inputs["w_att_out"][0], np.float32),
        "w_o": np.ascontiguousarray(inputs["w_o"][0], np.float32),
        "w_ff1": np.ascontiguousarray(inputs["w_ff1"][0], np.float32),
        "w_ff2": np.ascontiguousarray(inputs["w_ff2"][0], np.float32),
        "biasg": biasg, "cossin": cs, "rconst": rc,
    }
    in_maps = []
    B = min(B, int(_CACHE.get("ncores", B)))
    for b in range(B):
        mp = dict(shared)
        mp["xT"] = np.ascontiguousarray(x[b].T)
        mp["ccol"] = _col_layout(c[b])
        in_maps.append(mp)
    debug = bool(_CACHE.get("debug", False))
    nc = build_program(debug=debug, stop=_CACHE.get("stop"), skip=_CACHE.get("skip", ()))
    res = run_bass_kernel_spmd(nc, in_maps, core_ids=list(range(B)))
    _CACHE["last"] = res
    out = np.stack([np.asarray(res.results[b]["outT"], np.float32).T for b in range(B)], axis=0)
    return np.ascontiguousarray(out)
```

```python
import math
from contextlib import ExitStack

import numpy as np
import concourse.bass as bass
import concourse.mybir as mybir
from concourse.bass_utils import run_bass_kernel_spmd

F32 = mybir.dt.float32
BF16 = mybir.dt.bfloat16
AF = mybir.ActivationFunctionType
ALU = mybir.AluOpType

S = 2048
D = 1024
NB = 8
IN_COLS = 12800
GROUPS = ((128, 1), (512, 4), (2048, 16))

_ESZ = {str(F32): 4, str(BF16): 2, str(mybir.dt.int32): 4, str(mybir.dt.uint32): 4}
PSUM_BANK = 2048


def _region(ap):
    name = ap.tensor.name
    dims = ap.ap
    esz = _ESZ[str(ap.dtype)]
    off = int(ap.offset)
    space = str(ap.space).upper()
    if "DRAM" in space:
        ext = sum((c - 1) * abs(s) for s, c in dims) + 1
        return (name, 0, 1, off * esz, (off + ext) * esz)
    pstep, npart = dims[0]
    p0 = off // pstep if pstep else 0
    f0 = off % pstep if pstep else off
    ext = sum((c - 1) * abs(s) for s, c in dims[1:]) + 1
    blo, bhi = f0 * esz, (f0 + ext) * esz
    if "PSUM" in space:
        blo = (blo // PSUM_BANK) * PSUM_BANK
        bhi = -(-bhi // PSUM_BANK) * PSUM_BANK
    return (name, p0, p0 + npart, blo, bhi)


def _ovl(a, b):
    return a[1] < b[2] and b[1] < a[2] and a[3] < b[4] and b[3] < a[4]


def _covers(a, b):
    return a[1] <= b[1] and a[2] >= b[2] and a[3] <= b[3] and a[4] >= b[4]


class _Op:
    __slots__ = ("id", "eng", "fn", "dma", "deps", "signal", "token")


class Prog:
    ENGS = ("pe", "act", "dve", "pool", "sp")

    def __init__(self, nc, n_dma_sems=40):
        self.nc = nc
        self.ops = []
        self.eng_ops = {e: [] for e in self.ENGS}
        self.wrecs = {}
        self.rrecs = {}
        self.n_dma_sems = n_dma_sems
        self.out_dma_ops = []

    def op(self, eng, fn, reads=(), writes=(), dma=False, is_out=False, after=()):
        o = _Op()
        o.id = len(self.ops)
        o.eng = eng
        o.fn = fn
        o.dma = dma
        o.signal = False
        o.token = None
        deps = set()
        rregs = [_region(a) for a in reads]
        wregs = [_region(a) for a in writes]
        for r in rregs:
            for w in self.wrecs.get(r[0], ()):
                if _ovl(r, w[0]):
                    deps.add(w[1])
        for wr in wregs:
            for w in self.wrecs.get(wr[0], ()):
                if _ovl(wr, w[0]):
                    deps.add(w[1])
            for key, oid in self.rrecs.get(wr[0], {}).items():
                if _ovl(wr, key[1]):
                    deps.add(oid)
        for r in rregs:
            key = (("dma", o.id) if dma else eng, r)
            self.rrecs.setdefault(r[0], {})[key] = o.id
        for wr in wregs:
            lst = self.wrecs.setdefault(wr[0], [])
            lst[:] = [w for w in lst if not _covers(wr, w[0])]
            lst.append((wr, o.id))
            rd = self.rrecs.get(wr[0])
            if rd:
                for key in [k for k in rd if _covers(wr, k[1])]:
                    del rd[key]
        deps.update(after)
        deps.discard(o.id)
        best = {}
        red = set()
        for d in deps:
            dop = self.ops[d]
            if dop.dma:
                red.add(d)
            else:
                if dop.eng == eng and eng == "pe" and not dma:
                    continue
                if dop.eng not in best or best[dop.eng] < d:
                    best[dop.eng] = d
        red.update(best.values())
        o.deps = red
        self.ops.append(o)
        self.eng_ops[eng].append(o)
        if is_out:
            self.out_dma_ops.append(o.id)
        return o.id

    def dma(self, eng, out, in_, is_out=False, track_in=True, after=()):
        rd = [in_] if track_in else []
        return self.op(eng, lambda e, out=out, in_=in_: e.dma_start(out=out, in_=in_),
                       reads=rd, writes=[out], dma=True, is_out=is_out, after=after)

    def finalize(self, es):
        nc = self.nc
        fin = _Op()
        fin.id = len(self.ops)
        fin.eng = "sp"
        fin.fn = None
        fin.dma = False
        fin.signal = False
        fin.token = None
        fin.deps = set(self.out_dma_ops)
        self.ops.append(fin)
        self.eng_ops["sp"].append(fin)
        dma_ops = [o for o in self.ops if o.dma]
        N = self.n_dma_sems
        for j, o in enumerate(dma_ops):
            if j >= N:
                o.deps.add(dma_ops[j - N].id)
        for o in self.ops:
            for d in o.deps:
                self.ops[d].signal = True
        for o in dma_ops:
            o.signal = True
        self.eng_sem = {e: es.enter_context(nc.semaphore("s_" + e)) for e in self.ENGS}
        self.dma_sems = [es.enter_context(nc.semaphore("d%d" % i)) for i in range(N)]
        cnt = {e: 0 for e in self.ENGS}
        for j, o in enumerate(dma_ops):
            o.token = (self.dma_sems[j % N], 16 * (j // N + 1))
        for e in self.ENGS:
            for o in self.eng_ops[e]:
                if o.dma:
                    continue
                if o.signal:
                    cnt[e] += 1
                    o.token = (self.eng_sem[e], cnt[e])

    def emit(self, ename, e):
        known = {}
        for o in self.eng_ops[ename]:
            need = {}
            for d in o.deps:
                sem, val = self.ops[d].token
                k = id(sem)
                if k not in need or need[k][1] < val:
                    need[k] = (sem, val)
            for k, (sem, val) in need.items():
                if known.get(k, 0) < val:
                    e.wait_ge(sem, val)
                    known[k] = val
            if o.fn is None:
                continue
            ins = o.fn(e)
            if o.signal:
                ins.then_inc(o.token[0], 16 if o.dma else 1)

    def run_block(self):
        nc = self.nc
        with nc.Block() as block:
            @block.tensor
            def _(e):
                self.emit("pe", e)

            @block.scalar
            def _(e):
                self.emit("act", e)

            @block.vector
            def _(e):
                self.emit("dve", e)

            @block.gpsimd
            def _(e):
                self.emit("pool", e)

            @block.sync
            def _(e):
                self.emit("sp", e)


def _isap(x):
    return not isinstance(x, (int, float)) and x is not None


class _Stop(Exception):
    pass


def build_program(debug=False, stop=None, skip=()):
    nc = bass.Bass("TRN2", target_bir_lowering=False)
    dt_in = lambda name, shape, dt=F32: nc.dram_tensor(name, list(shape), dt, kind="ExternalInput").ap()
    xT_d = dt_in("xT", [D, S])
    ccol_d = dt_in("ccol", [128, 8])
    vecs_d = dt_in("vecs", [128, 104])
    w_ada_d = dt_in("w_ada", [D, 6 * D])
    w_in_d = dt_in("w_in", [D, IN_COLS])
    w_ro_d = dt_in("w_ret_out", [2048, D])
    w_ao_d = dt_in("w_att_out", [512, D])
    w_o_d = dt_in("w_o", [D, D])
    w_f1_d = dt_in("w_ff1", [D, 4 * D])
    w_f2_d = dt_in("w_ff2", [4 * D, D])
    biasg_d = dt_in("biasg", [128, 12, 256])
    cs_d = dt_in("cossin", [128, 2, S])
    rconst_d = dt_in("rconst", [128, 4 * 128 + 4 * 128 + 4 + 256 + 128])
    outT_d = nc.dram_tensor("outT", [D, S], F32, kind="ExternalOutput").ap()
    retT_d = nc.dram_tensor("retT_scr", [2048, S], BF16, kind="ExternalOutput").ap()
    dbg = {}
    if debug:
        dbg["hT"] = nc.dram_tensor("dbg_hT", [128, 8, S], BF16, kind="ExternalOutput").ap()
        dbg["attT"] = nc.dram_tensor("dbg_attT", [128, 4, S], BF16, kind="ExternalOutput").ap()
        dbg["retT"] = nc.dram_tensor("dbg_retT", [128, 16, S], BF16, kind="ExternalOutput").ap()
        dbg["x1T"] = nc.dram_tensor("dbg_x1T", [128, 8, S], F32, kind="ExternalOutput").ap()
        dbg["modT"] = nc.dram_tensor("dbg_modT", [128, 48], F32, kind="ExternalOutput").ap()

    with ExitStack() as es:
        P = Prog(nc)
        sb = lambda name, shape, dt: es.enter_context(nc.sbuf_tensor("sb_" + name, list(shape), dt))
        vecs = sb("vecs", [128, 104], F32)
        modT = sb("modT", [128, 48], F32)
        cols = sb("cols", [128, 64], F32)
        rconst = sb("rconst", [128, 4 * 128 + 4 * 128 + 4 + 256 + 128], F32)
        cbf = sb("cbf", [128, 4 * 128 + 128 + 128 + 8], BF16)
        hT = sb("hT", [128, 8, S], BF16)
        attT = sb("attT", [128, 4, S], BF16)
        NST, NRING = 2, 3
        wstage = [sb("wst%d" % i, [128, 2048], F32) for i in range(NST)]
        wring = [sb("wrg%d" % i, [128, 4096], BF16) for i in range(NRING)]
        ARENA_B = 104 * 1024
        arena = sb("arena", [128, ARENA_B // 2], BF16)
        psum = es.enter_context(nc.psum_tensor("ps", [128, 4096], F32))

        def bank(i):
            return psum[:, i * 512:(i + 1) * 512]

        class Arena:
            def __init__(self):
                self.off = 0

            def reset(self, off=0):
                self.off = off

            def get(self, n, dt):
                esz = 4 if dt == F32 else 2
                self.off = -(-self.off // 64) * 64
                o = self.off
                self.off += n * esz
                assert self.off <= ARENA_B, ("arena overflow", self.off)
                v = arena[:, o // 2:(o + n * esz) // 2]
                return v.bitcast(F32) if dt == F32 else v

        A = Arena()

        def mm(out, lhsT, rhs, start, stop, skip=False):
            P.op("pe", lambda e: e.matmul(out, lhsT=lhsT, rhs=rhs, start=start, stop=stop, skip_group_check=skip),
                 reads=[lhsT, rhs], writes=[out])

        def tr(out, in_):
            P.op("pe", lambda e: e.transpose(out=out, in_=in_, identity=ident_bf), reads=[in_, ident_bf], writes=[out])

        def act(out, in_, func, scale=1.0, bias=None):
            rd = [in_] + [a for a in (scale, bias) if _isap(a)]
            kw = dict(out=out, in_=in_, func=func, scale=scale)
            if bias is not None:
                kw["bias"] = bias
            P.op("act", lambda e: e.activation(**kw), reads=rd, writes=[out])

        def tt(eng, out, in0, in1, op):
            P.op(eng, lambda e: e.tensor_tensor(out=out, in0=in0, in1=in1, op=op), reads=[in0, in1], writes=[out])

        def ts(eng, out, in0, s1, s2, op0, op1=None):
            rd = [in0] + [a for a in (s1, s2) if _isap(a)]
            if op1 is None:
                P.op(eng, lambda e: e.tensor_scalar(out=out, in0=in0, scalar1=s1, scalar2=None, op0=op0), reads=rd, writes=[out])
            else:
                P.op(eng, lambda e: e.tensor_scalar(out=out, in0=in0, scalar1=s1, scalar2=s2, op0=op0, op1=op1), reads=rd, writes=[out])

        def stt(out, in0, scalar, in1, op0, op1):
            rd = [in0, in1] + ([scalar] if _isap(scalar) else [])
            P.op("dve", lambda e: e.scalar_tensor_tensor(out=out, in0=in0, scalar=scalar, in1=in1, op0=op0, op1=op1),
                 reads=rd, writes=[out])

        def cp(eng, out, in_):
            if eng == "act":
                act(out, in_, AF.Copy)
            else:
                P.op(eng, lambda e: e.tensor_copy(out=out, in_=in_), reads=[in_], writes=[out])

        def recip(out, in_):
            P.op("dve", lambda e: e.reciprocal(out=out, in_=in_), reads=[in_], writes=[out])

        evac_rr = [0]

        def evac(out, in_):
            evac_rr[0] ^= 1
            cp("act" if evac_rr[0] else "dve", out, in_)

        wstate = {"s": 0, "r": 0, "o": 0}

        def wload(W, r0, R, c0, C, share=False, engs=("pool",)):
            if share:
                so = wstate["o"]
                assert so + R * C <= 4096
                slot = wring[(wstate["r"] - 1) % len(wring)]
            else:
                assert R * C <= 4096
                so = 0
                slot = wring[wstate["r"] % len(wring)]
                wstate["r"] += 1
            wstate["o"] = so + R * C
            view = slot[:, so:so + R * C].rearrange("p (r c) -> p r c", r=R)
            rpp = max(1, 2048 // C)
            for r in range(0, R, rpp):
                rr = min(rpp, R - r)
                st = wstage[wstate["s"] % NST]
                wstate["s"] += 1
                stv = st[:, 0:rr * C].rearrange("p (r c) -> p r c", r=rr)
                src = W[(r0 + r) * 128:(r0 + r + rr) * 128, c0:c0 + C].rearrange("(r p) c -> p r c", p=128)
                P.dma("sp", stv, src, track_in=False)
                cp(engs[wstate["s"] % len(engs)], view[:, r:r + rr, :], stv)
            return view

        try:
            P.dma("sp", vecs[:], vecs_d, track_in=False)
            P.dma("sp", rconst[:], rconst_d, track_in=False)
            OFF_DEC, OFF_QD, OFF_KD, OFF_MASK, OFF_ID = 0, 512, 1024, 1028, 1284
            decay_bf = cbf[:, 0:512]
            ident_bf = cbf[:, 512:640]
            ones_bf = cbf[:, 640:768]
            cp("dve", decay_bf, rconst[:, OFF_DEC:OFF_DEC + 512])
            cp("dve", ident_bf, rconst[:, OFF_ID:OFF_ID + 128])
            P.op("dve", lambda e: e.memset(ones_bf, 1.0), writes=[ones_bf])
            qd_f = rconst[:, OFF_QD:OFF_QD + 512]
            kdec = rconst[:, OFF_KD:OFF_KD + 4]
            mask256 = rconst[:, OFF_MASK:OFF_MASK + 256]
            a1 = cols[:, 0:8]
            a2 = cols[:, 8:16]
            g1h = cols[:, 16:24]
            P.op("dve", lambda e: e.memset(cols[:, 24:25], 1e-6), writes=[cols[:, 24:25]])
            P.op("dve", lambda e: e.memset(cols[:, 25:26], 256.0 * 1e-5), writes=[cols[:, 25:26]])
            P.op("dve", lambda e: e.memset(cols[:, 26:27], -0.5), writes=[cols[:, 26:27]])
            eps_rms = cols[:, 24:25]
            eps_gn = cols[:, 25:26]
            mhalf = cols[:, 26:27]
            n1g, n2g, nfg = vecs[:, 48:56], vecs[:, 56:64], vecs[:, 64:72]

            A.reset()
            cc = A.get(8, F32)
            ct = A.get(8, F32)
            sbf = cbf[:, 768:776]
            P.dma("sp", cc, ccol_d, track_in=False)
            act(ct, cc, AF.Tanh, scale=0.5)
            ts("dve", ct, ct, 0.5, 0.5, ALU.mult, ALU.add)
            tt("dve", sbf, ct, cc, ALU.mult)
            sh1, sc1, g1c, sh2, sc2, g2c = [modT[:, i * 8:(i + 1) * 8] for i in range(6)]

            def mod_units(u0, u1, pmod):
                for u in range(u0, u1):
                    wv_ = wload(w_ada_d, 0, 8, u * 256, 256, engs=("act",))
                    for jj in range(2):
                        j = 2 * u + jj
                        for k in range(8):
                            mm(pmod[:, j:j + 1], wv_[:, k, jj * 128:(jj + 1) * 128], sbf[:, k:k + 1], k == 0, k == 7)
                tt("dve", modT[:, 2 * u0:2 * u1], pmod[:, 2 * u0:2 * u1], vecs[:, 2 * u0:2 * u1], ALU.add)

            def rms_stats(src_tile, pb):
                sq = A.get(8 * 512, BF16).rearrange("p (k t) -> p k t", k=8)
                act(sq, src_tile, AF.Square)
                for k in range(8):
                    mm(pb, ones_bf, sq[:, k, :], k == 0, k == 7)
                rs = A.get(512, F32)
                act(rs, pb, AF.Sqrt, scale=1.0 / D, bias=eps_rms)
                rstd = A.get(512, F32)
                recip(rstd, rs)
                return rstd

            mod_units(0, 8, bank(0)[:, 0:48])
            stt(a1, sc1, 1.0, n1g, ALU.add, ALU.mult)
            xts, rstds = [], []
            A.reset(1024)
            xids = []
            for tg in range(4):
                xt = A.get(8 * 512, F32).rearrange("p (k t) -> p k t", k=8)
                xids.append(P.dma("sp", xt, xT_d[:, tg * 512:(tg + 1) * 512].rearrange("(k p) t -> p k t", p=128),
                                  track_in=False, after=xids[-2:-1] if len(xids) >= 2 else ()))
                xts.append(xt)
            n1base = A.off
            for tg in range(4):
                A.reset(n1base + (tg % 2) * 16 * 1024)
                rstd = rms_stats(xts[tg], bank(1 + tg % 2))
                tt("dve", xts[tg], xts[tg], rstd.unsqueeze(1).broadcast_to([128, 8, 512]), ALU.mult)
                for k in range(8):
                    act(hT[:, k, tg * 512:(tg + 1) * 512], xts[tg][:, k, :], AF.Identity,
                        scale=a1[:, k:k + 1], bias=sh1[:, k:k + 1])
            if debug:
                P.dma("sp", dbg["hT"], hT[:], is_out=True)
            if stop == "A2":
                raise _Stop()

            A.reset()
            wv_units = [wload(w_in_d, 0, 8, 6144 + g_ * 1536 + 1024, 512) for g_ in range(3)]
            expb = A.get(12 * 256, BF16).rearrange("p (g c) -> p g c", g=12)
            v_all = [A.get(16 * 512, BF16).rearrange("p (b c) -> p b c", b=16) for _ in range(3)]
            qTg = [A.get(S, BF16) for _ in range(3)]
            kTg = [A.get(S, BF16) for _ in range(3)]
            au_off = -(-A.off // 64) * 64
            attU = A.get(S, F32)
            denU = A.get(S, F32)
            ebuf = [A.get(256, BF16) for _ in range(3)]
            pbuf = [A.get(256, BF16) for _ in range(3)]
            bg = arena[:, au_off // 2:au_off // 2 + 2 * 12 * 256].bitcast(F32).rearrange("p (g c) -> p g c", g=12)
            den_tmp = arena[:, au_off // 2 + 2 * 12 * 256:au_off // 2 + 2 * 12 * 256 + 512].bitcast(F32)
            P.dma("sp", bg, biasg_d, track_in=False)
            for gh in range(12):
                tt("dve", bg[:, gh, :], bg[:, gh, :], mask256, ALU.mult)
            ts("dve", den_tmp, mask256, 30000.0, -30000.0, ALU.mult, ALU.add)
            for gh in range(12):
                tt("dve", expb[:, gh, :], bg[:, gh, :], den_tmp, ALU.add)
            pj = [0]

            def pbank():
                pj[0] ^= 1
                return bank(pj[0])

            for g, (win, dil) in enumerate(GROUPS):
                base = 6144 + g * 1536
                wv = wv_units[g]
                nsub = 16 // dil
                for bi in range(16):
                    r, n = bi // nsub, bi % nsub
                    t0 = r + dil * 128 * n
                    pb = pbank()
                    for k in range(8):
                        mm(pb, hT[:, k, t0:t0 + 127 * dil + 1:dil], wv[:, k, :], k == 0, k == 7)
                    evac(v_all[g][:, bi, :], pb)
            mod_next = [8]

            def mod_some(n):
                for _ in range(n):
                    u = mod_next[0]
                    if u >= 24:
                        return
                    mod_next[0] += 1
                    pb_ = pbank()
                    wv_ = wload(w_ada_d, 0, 8, u * 256, 256, engs=("act", "dve"))
                    for jj in range(2):
                        for k in range(8):
                            mm(pb_[:, jj:jj + 1], wv_[:, k, jj * 128:(jj + 1) * 128], sbf[:, k:k + 1], k == 0, k == 7)
                    tt("dve", modT[:, 2 * u:2 * u + 2], pb_[:, 0:2], vecs[:, 2 * u:2 * u + 2], ALU.add)
            if stop == "B0":
                raise _Stop()
            scale_q = 1.0 / math.sqrt(128.0)
            tile_ctr = [0]
            for h in range(0 if "B" in skip else 4):
                for g, (win, dil) in enumerate(GROUPS):
                    base = 6144 + g * 1536
                    L = S // dil
                    nsub = 16 // dil
                    wq = wload(w_in_d, 0, 8, base + h * 128, 128)
                    wk = wload(w_in_d, 0, 8, base + 512 + h * 128, 128, share=True)
                    for (wt, dst, scl) in ((wq, qTg[g], scale_q), (wk, kTg[g], 1.0)):
                        for tg in range(4):
                            pb = pbank()
                            for k in range(8):
                                mm(pb, wt[:, k, :], hT[:, k, tg * 512:(tg + 1) * 512], k == 0, k == 7)
                            if dil == 1:
                                act(dst[:, tg * 512:(tg + 1) * 512], pb, AF.Copy, scale=scl)
                            else:
                                w_ = 512 // dil
                                src_v = pb.rearrange("p (i r) -> p r i", r=dil)
                                dst_v = dst.rearrange("p (r l) -> p r l", r=dil)[:, :, tg * w_:(tg + 1) * w_]
                                act(dst_v, src_v, AF.Copy, scale=scl)
                    mod_some(2 if (h * 3 + g) < 4 else 1)
                    alltiles = []
                    for R in range(4):
                        tiles = []
                        if dil == 1:
                            for n in range(max(0, 4 * R - 1), 4 * R + 4):
                                qs = [q for q in (n, n + 1) if 4 * R <= q <= 4 * R + 3]
                                tiles.append((n, n, qs[0], len(qs)))
                        elif dil == 4:
                            for n in range(4):
                                qs = [q for q in (n, n + 1) if q <= 3]
                                tiles.append((R * 4 + n, n, R * 4 + qs[0], len(qs)))
                        else:
                            for r in range(4 * R, 4 * R + 4):
                                tiles.append((r, 0, r, 1))
                        for ti, t_ in enumerate(tiles):
                            alltiles.append((R, ti == 0, ti == len(tiles) - 1) + t_)

                    def stage1(t_):
                        R, first, last, kb, kn, qlo, nq = t_
                        W_ = nq * 128
                        sb_ = bank(2 + tile_ctr[0] % 2)[:, 0:W_]
                        pm = pbuf[tile_ctr[0] % 3][:, 0:W_]
                        tile_ctr[0] += 1
                        qn_lo = qlo - (kb - kn)
                        moff = (qn_lo - kn) * 128
                        mm(sb_, kTg[g][:, kb * 128:(kb + 1) * 128], qTg[g][:, qlo * 128:qlo * 128 + W_], True, False)
                        mm(sb_, ident_bf, expb[:, g * 4 + h, moff:moff + W_], False, True)
                        act(pm, sb_, AF.Exp)
                        return pm

                    def stage2(t_, pm):
                        R, first, last, kb, kn, qlo, nq = t_
                        W_ = nq * 128
                        oacc = bank(4 + 2 * (R % 2))
                        dacc = bank(5 + 2 * (R % 2))
                        c0 = (qlo - 4 * R) * 128
                        mm(oacc[:, c0:c0 + W_], v_all[g][:, kb, h * 128:(h + 1) * 128], pm, first, last, skip=True)
                        mm(dacc[:, c0:c0 + W_], ones_bf, pm, first, last, skip=True)
                        if not last:
                            return
                        if dil == 1:
                            cp("dve", attU[:, R * 512:(R + 1) * 512], oacc)
                            cp("dve", denU[:, R * 512:(R + 1) * 512], dacc)
                        elif dil == 4:
                            va = attU[:, R:S:4]
                            vd = denU[:, R:S:4]
                            tt("dve", va, oacc, va, ALU.add)
                            tt("dve", vd, dacc, vd, ALU.add)
                        else:
                            va = attU.rearrange("p (i r) -> p r i", r=16)[:, 4 * R:4 * R + 4, :]
                            vd = denU.rearrange("p (i r) -> p r i", r=16)[:, 4 * R:4 * R + 4, :]
                            po = oacc.rearrange("p (a i) -> p a i", a=4)
                            pd = dacc.rearrange("p (a i) -> p a i", a=4)
                            tt("dve", va, po, va, ALU.add)
                            tt("dve", vd, pd, vd, ALU.add)

                    prev = None
                    for t_ in alltiles:
                        pm = stage1(t_)
                        if prev is not None:
                            stage2(*prev)
                        prev = (t_, pm)
                    stage2(*prev)
                recip(denU, denU)
                tt("dve", attT[:, h, :], attU, denU, ALU.mult)
            mod_some(24)
            stt(a2, sc2, 1.0, n2g, ALU.add, ALU.mult)
            ts("dve", g1h, g1c, 0.5, None, ALU.mult)
            if debug:
                P.dma("sp", dbg["modT"], modT[:], is_out=True)
            if debug:
                P.dma("sp", dbg["attT"], attT[:], is_out=True)
            if stop == "B":
                raise _Stop()

            A.reset()
            cst = [A.get(2 * 512, F32).rearrange("p (a t) -> p a t", a=2) for _ in range(2)]
            qT = A.get(2 * S, BF16).rearrange("p (c t) -> p c t", c=2)
            qTd = A.get(2 * S, BF16).rearrange("p (c t) -> p c t", c=2)
            kT = A.get(2 * S, BF16).rearrange("p (c t) -> p c t", c=2)
            ktok = A.get(16 * 256, BF16).rearrange("p (b c) -> p b c", b=16)
            vtok = A.get(4 * 512, BF16).rearrange("p (b c) -> p b c", b=4)
            sgT = A.get(4 * 1024, BF16).rearrange("p (f t) -> p f t", f=4)
            Sst = A.get(2 * 512, F32).rearrange("p (c e) -> p c e", c=2)
            Sbf = A.get(2 * 512, BF16).rearrange("p (c e) -> p c e", c=2)
            rt = [A.get(512, F32) for _ in range(4)]
            PTb = [A.get(128, BF16) for _ in range(2)]
            ybuf = [A.get(512, BF16) for _ in range(3)]
            stats = [A.get(16, F32) for _ in range(3)]
            affb = [A.get(512, BF16).rearrange("p (f t) -> p f t", f=4) for _ in range(2)]
            rstage = [A.get(4 * 512, BF16).rearrange("p (f t) -> p f t", f=4) for _ in range(2)]
            gnc = cols[:, 32:64]
            ts("dve", gnc, vecs[:, 72:104], 0.5, None, ALU.mult)
            csi = [0]
            rti = [0]
            pjc = [0]

            def pjbank():
                pjc[0] ^= 1
                return bank(1 if pjc[0] else 7)

            NH = 0 if "C" in skip else 4

            def load_qk(h_):
                wq_ = wload(w_in_d, 0, 8, h_ * 256, 256)
                wk_ = wload(w_in_d, 0, 8, 1024 + h_ * 256, 256, share=True)
                return wq_, wk_

            nxt_qk = load_qk(0) if NH else None
            for h in range(NH):
                wq, wk = nxt_qk
                for tg in range(4):
                    tsl = slice(tg * 512, (tg + 1) * 512)
                    cs = cst[csi[0] % 2]
                    csi[0] += 1
                    P.dma("sp", cs, cs_d[:, :, tsl], track_in=False)
                    cosv, sinv = cs[:, 0, :], cs[:, 1, :]
                    for (wt, dst) in ((wq, qT), (wk, kT)):
                        pr = rti[0] % 2
                        rti[0] += 1
                        p1, p2 = bank(2 * pr), bank(2 * pr + 1)
                        r0, r1, r2, r3 = rt[0:4]
                        for k in range(8):
                            mm(p1, wt[:, k, 0:128], hT[:, k, tsl], k == 0, k == 7)
                        for k in range(8):
                            mm(p2, wt[:, k, 128:256], hT[:, k, tsl], k == 0, k == 7)
                        tt("dve", r0, p1, cosv, ALU.mult)
                        tt("dve", r1, p2, sinv, ALU.mult)
                        tt("dve", r2, p1, sinv, ALU.mult)
                        tt("dve", r3, p2, cosv, ALU.mult)
                        tt("dve", dst[:, 0, tsl], r0, r1, ALU.subtract)
                        tt("dve", dst[:, 1, tsl], r2, r3, ALU.add)
                wv = wload(w_in_d, 0, 8, 2048 + h * 512, 512, engs=("pool", "act"))
                wg = wload(w_in_d, 0, 8, 4096 + h * 512, 512, engs=("pool", "act"))
                qdb = qd_f[:, h * 128:(h + 1) * 128]
                qdb16 = qdb.unsqueeze(1).broadcast_to([128, 16, 128])
                for c in range(2):
                    tt("pool", qTd[:, c, :].rearrange("p (b i) -> p b i", b=16),
                       qT[:, c, :].rearrange("p (b i) -> p b i", b=16), qdb16, ALU.mult)
                for q4 in range(4):
                    pbt = bank(4 + q4 % 2).bitcast(BF16)
                    for cj in range(4):
                        ci = q4 * 4 + cj
                        for c in range(2):
                            tr(pbt[:, cj * 256 + c * 128:cj * 256 + (c + 1) * 128], kT[:, c, ci * 128:(ci + 1) * 128])
                    act(ktok[:, q4 * 4:(q4 + 1) * 4, :], pbt.rearrange("p (b c) -> p b c", b=4), AF.Identity,
                        scale=kdec[:, h:h + 1])

                def vproj(ci):
                    pb = pjbank()
                    for k in range(8):
                        mm(pb, hT[:, k, ci * 128:(ci + 1) * 128], wv[:, k, :], k == 0, k == 7)
                    evac(vtok[:, ci % 4, :], pb)

                def gate_tile(fc, tg):
                    tsl = slice(tg * 512, (tg + 1) * 512)
                    pg = pjbank()
                    th = rt[(fc + tg) % 4]
                    for k in range(8):
                        mm(pg, wg[:, k, fc * 128:(fc + 1) * 128], hT[:, k, tsl], k == 0, k == 7)
                    act(th, pg, AF.Tanh, scale=0.5)
                    stt(sgT[:, fc, (tg % 2) * 512:(tg % 2 + 1) * 512], th, 1.0, pg, ALU.add, ALU.mult)

                vproj(0)
                for fc in range(4):
                    gate_tile(fc, 0)
                cdh = math.exp(math.log1p(-(2.0 ** (-5.0 - h))) * 128.0)
                gcol = gnc[:, h * 4:(h + 1) * 4]
                bcol = gnc[:, 16 + h * 4:16 + (h + 1) * 4]

                def st1(ci):
                    csl = slice(ci * 128, (ci + 1) * 128)
                    ps_s = bank(0)[:, 0:128]
                    for c in range(2):
                        mm(ps_s, kT[:, c, csl], qT[:, c, csl], c == 0, c == 1)
                    tt("dve", PTb[ci % 2], ps_s, decay_bf[:, h * 128:(h + 1) * 128], ALU.mult)

                def st2(ci):
                    csl = slice(ci * 128, (ci + 1) * 128)
                    po = bank(2 + ci % 2)
                    vt = vtok[:, ci % 4, :]
                    mm(po, PTb[ci % 2], vt, True, ci == 0)
                    if ci > 0:
                        for c in range(2):
                            mm(po, qTd[:, c, csl], Sbf[:, c, :], False, c == 1)
                    if ci < 15:
                        for c in range(2):
                            pu = bank(4 + c)
                            mm(pu, ktok[:, ci, c * 128:(c + 1) * 128], vt, True, True)
                            if ci == 0:
                                cp("dve", Sst[:, c, :], pu)
                            else:
                                stt(Sst[:, c, :], Sst[:, c, :], cdh, pu, ALU.mult, ALU.add)
                            act(Sbf[:, c, :], Sst[:, c, :], AF.Copy)
                    st = stats[ci % 3]
                    P.op("dve", lambda e, st=st, po=po: e.bn_stats(out=st[:, 0:6], in_=po), reads=[po], writes=[st[:, 0:6]])
                    P.op("dve", lambda e, st=st: e.bn_aggr(out=st[:, 6:8], in_=st[:, 0:6]), reads=[st[:, 0:6]], writes=[st[:, 6:8]])
                    ts("dve", st[:, 8:9], st[:, 7:8], eps_gn, None, ALU.add)
                    tt("pool", st[:, 9:10], st[:, 8:9], mhalf, ALU.pow)
                    ts("dve", st[:, 10:11], st[:, 6:7], st[:, 9:10], -1.0, ALU.mult, ALU.mult)
                    act(ybuf[ci % 3], po, AF.Identity, scale=st[:, 9:10], bias=st[:, 10:11])

                def st3(ci):
                    y = ybuf[ci % 3]
                    ptb = bank(6).bitcast(BF16)[:, 0:512].rearrange("p (f t) -> p f t", f=4)
                    for fc in range(4):
                        tr(ptb[:, fc, :], y[:, fc * 128:(fc + 1) * 128])
                    af = affb[ci % 2]
                    for fc in range(4):
                        act(af[:, fc, :], ptb[:, fc, :], AF.Identity, scale=gcol[:, fc:fc + 1], bias=bcol[:, fc:fc + 1])
                    rs_ = rstage[(ci // 4) % 2]
                    tgs = (ci // 4) % 2
                    tt("dve", rs_[:, :, (ci % 4) * 128:(ci % 4 + 1) * 128], af,
                       sgT[:, :, tgs * 512 + (ci % 4) * 128:tgs * 512 + (ci % 4 + 1) * 128], ALU.mult)
                    if ci % 4 == 3:
                        q4 = ci // 4
                        dst = retT_d[h * 512:(h + 1) * 512, q4 * 512:(q4 + 1) * 512].rearrange("(f p) t -> p f t", p=128)
                        P.dma("sp", dst, rs_)

                nxt_qk = load_qk(h + 1) if h + 1 < NH else None
                for i in range(18):
                    if i < 16:
                        st1(i)
                    if i + 1 < 16:
                        vproj(i + 1)
                    if 2 <= i < 14:
                        gate_tile((i - 2) % 4, (i - 2) // 4 + 1)
                    if 1 <= i <= 16:
                        st2(i - 1)
                    if i >= 2:
                        st3(i - 2)

            if stop == "C":
                raise _Stop()
            A.reset()
            mergedT = A.get(8 * S, BF16).rearrange("p (c t) -> p c t", c=8)
            big = A.off
            retT = A.get(16 * 1024, BF16).rearrange("p (k t) -> p k t", k=16)
            tmpm = [A.get(512, F32) for _ in range(4)]
            mi = [0]
            for th_ in range(2):
                for k in range(16):
                    P.dma("sp", retT[:, k, :], retT_d[k * 128:(k + 1) * 128, th_ * 1024:(th_ + 1) * 1024])
                if debug:
                    P.dma("sp", dbg["retT"][:, :, th_ * 1024:(th_ + 1) * 1024], retT, is_out=True)
                for c in range(8):
                    wro = wload(w_ro_d, 0, 16, c * 128, 128, engs=("pool", "act"))
                    wao = wload(w_ao_d, 0, 4, c * 128, 128, share=True)
                    wgab = wload(w_in_d, 0, 8, 10752 + c * 128, 128, engs=("pool", "act"))
                    wgb = wload(w_in_d, 0, 8, 11776 + c * 128, 128)
                    pAs = []
                    for tgl in range(2):
                        lsl = slice(tgl * 512, (tgl + 1) * 512)
                        pA = bank(tgl)
                        for k in range(16):
                            mm(pA, wro[:, k, :], retT[:, k, lsl], k == 0, k == 15)
                        pAs.append(pA)
                    for tgl in range(2):
                        tg = th_ * 2 + tgl
                        tsl = slice(tg * 512, (tg + 1) * 512)
                        pA = pAs[tgl]
                        pB, pC, pD = [bank(2 + tgl * 3 + i) for i in range(3)]
                        for k in range(4):
                            mm(pB, wao[:, k, :], attT[:, k, tsl], k == 0, k == 3)
                        for k in range(8):
                            mm(pC, wgab[:, k, :], hT[:, k, tsl], k == 0, k == 7)
                        for k in range(8):
                            mm(pD, wgb[:, k, :], hT[:, k, tsl], k == 0, k == 7)
                        sa, sb2 = tmpm[(mi[0] % 2) * 2], tmpm[(mi[0] % 2) * 2 + 1]
                        mi[0] += 1
                        act(sa, pC, AF.Tanh, scale=0.5)
                        act(sb2, pD, AF.Tanh, scale=0.5)
                        stt(sa, sa, 1.0, pA, ALU.add, ALU.mult)
                        stt(sb2, sb2, 1.0, pB, ALU.add, ALU.mult)
                        tt("dve", mergedT[:, c, tsl], sa, sb2, ALU.add)

            A.reset(big)
            x1T = A.get(8 * S, F32).rearrange("p (c t) -> p c t", c=8)
            xtmp = [A.get(512, F32) for _ in range(3)]
            xi = [0]
            wo_units = [wload(w_o_d, 0, 8, hf * 512, 512) for hf in range(2)]
            w1_first = wload(w_f1_d, 0, 8, 0, 512)
            fbase = A.off - 3 * 2048

            def phase_e(tg):
                tsl = slice(tg * 512, (tg + 1) * 512)
                for c2 in range(8):
                    wo = wo_units[c2 // 4][:, :, (c2 % 4) * 128:(c2 % 4 + 1) * 128]
                    pb = pbank()
                    for k in range(8):
                        mm(pb, wo[:, k, :], mergedT[:, k, tsl], k == 0, k == 7)
                    xt = xtmp[xi[0] % 3]
                    xi[0] += 1
                    P.dma("sp", xt, xT_d[c2 * 128:(c2 + 1) * 128, tsl], track_in=False)
                    stt(x1T[:, c2, tsl], pb, g1h[:, c2:c2 + 1], xt, ALU.mult, ALU.add)

            def phase_f(tg):
                tsl = slice(tg * 512, (tg + 1) * 512)
                rstd = rms_stats_small(x1T[:, :, tsl], bank(2 + tg % 2))
                for k in range(8):
                    t = ftmp[k % 4]
                    tt("dve", t, x1T[:, k, tsl], rstd, ALU.mult)
                    act(hT[:, k, tsl], t, AF.Identity, scale=a2[:, k:k + 1], bias=sh2[:, k:k + 1])

            def rms_stats_small(src_tile, pb):
                for k in range(8):
                    sq = fsq[k % 2]
                    act(sq, src_tile[:, k, :], AF.Square)
                    mm(pb, ones_bf, sq, k == 0, k == 7)
                act(frs, pb, AF.Sqrt, scale=1.0 / D, bias=eps_rms)
                recip(frs, frs)
                return frs

            fsq = [attT[:, 0, 0:512], attT[:, 0, 512:1024]]
            frs = attT[:, 0, 1024:2048].bitcast(F32)
            ftmp = [attT[:, 1 + i // 2, (i % 2) * 1024:(i % 2 + 1) * 1024].bitcast(F32) for i in range(4)]
            for tg in range(5):
                if tg < 4:
                    phase_e(tg)
                if tg >= 1:
                    phase_f(tg - 1)
            if debug:
                P.dma("sp", dbg["x1T"], x1T, is_out=True)
            if stop == "E":
                raise _Stop()

            A.reset(0)
            uT = A.get(8 * S, BF16).rearrange("p (f t) -> p f t", f=8)
            assert A.off <= big
            A.reset(fbase)
            rl = [A.get(512, BF16) for _ in range(3)]
            ri = [0]
            for fg in range(4):
                for half in range(2):
                    w1 = w1_first if (fg == 0 and half == 0) else wload(w_f1_d, 0, 8, fg * 1024 + half * 512, 512)
                    for tg in range(4):
                        tsl = slice(tg * 512, (tg + 1) * 512)
                        for fj in range(4):
                            fi = half * 4 + fj
                            pb = pbank()
                            for k in range(8):
                                mm(pb, w1[:, k, fj * 128:(fj + 1) * 128], hT[:, k, tsl], k == 0, k == 7)
                            r_ = rl[ri[0] % 3]
                            ri[0] += 1
                            ts("dve", r_, pb, 0.0, None, ALU.max)
                            act(uT[:, fi, tsl], r_, AF.Square)
                for half in range(2):
                    w2 = wload(w_f2_d, fg * 8, 8, half * 512, 512)
                    for cj in range(4):
                        c = half * 4 + cj
                        for tg in range(4):
                            tsl = slice(tg * 512, (tg + 1) * 512)
                            pb = bank(2 + (cj * 4 + tg) % 2)
                            for k in range(8):
                                mm(pb, w2[:, k, cj * 128:(cj + 1) * 128], uT[:, k, tsl], k == 0, k == 7)
                            stt(x1T[:, c, tsl], pb, g2c[:, c:c + 1], x1T[:, c, tsl], ALU.mult, ALU.add)

            A.reset(0)
            hbufs = [
                (A.get(8 * 512, BF16).rearrange("p (k t) -> p k t", k=8), A.get(512, F32),
                 A.get(8 * 512, F32).rearrange("p (k t) -> p k t", k=8)),
                (hT[:, 0:2, :].rearrange("p a (b t) -> p (a b) t", b=4), hT[:, 2, 0:1024].bitcast(F32),
                 hT[:, 4:8, :].bitcast(F32).rearrange("p a (b t) -> p (a b) t", b=2)),
            ]
            for tg in range(4):
                tsl = slice(tg * 512, (tg + 1) * 512)
                sq, rs, ost = hbufs[tg % 2]
                pbf = bank(4 + tg % 2)
                act(sq, x1T[:, :, tsl], AF.Square)
                for k in range(8):
                    mm(pbf, ones_bf, sq[:, k, :], k == 0, k == 7)
                act(rs, pbf, AF.Sqrt, scale=1.0 / D, bias=eps_rms)
                recip(rs, rs)
                for k in range(8):
                    stt(ost[:, k, :], x1T[:, k, tsl], nfg[:, k:k + 1], rs, ALU.mult, ALU.mult)
                P.dma("sp", outT_d[:, tsl].rearrange("(k p) t -> p k t", p=128), ost, is_out=True)


        except _Stop:
            pass
        P.finalize(es)
        P.run_block()
    return nc


def _t5_bucket_np(dist):
    max_exact = 16
    d_f = np.maximum(dist, 1).astype(np.float32)
    large = max_exact + (np.log(d_f / np.float32(max_exact)) / np.float32(math.log(2048 / max_exact))
                         * np.float32(32 - max_exact)).astype(np.int32)
    large = np.minimum(large, 31)
    return np.where(dist < max_exact, dist, large)


def _col_layout(v):
    v = np.asarray(v, np.float32).reshape(-1, 128)
    return np.ascontiguousarray(v.T)


def _constants():
    i = np.arange(128, dtype=np.float64)
    rc = np.zeros((128, 4 * 128 + 4 * 128 + 4 + 256 + 128), np.float32)
    for h in range(4):
        lg = math.log1p(-(2.0 ** (-5.0 - h)))
        rel = i[None, :] - i[:, None]
        rc[:, h * 128:(h + 1) * 128] = np.where(rel >= 0, np.exp(lg * np.maximum(rel, 0.0)), 0.0)
        rc[:, 512 + h * 128:512 + (h + 1) * 128] = np.exp(lg * (i + 1.0))[None, :]
        rc[:, 1024 + h] = np.exp(lg * (127.0 - i))
    kj = np.arange(128)[:, None]
    qq = np.arange(256)[None, :]
    m = qq - kj
    rc[:, 1028:1284] = ((m >= 0) & (m <= 128)).astype(np.float32)
    rc[:, 1284:1412] = np.eye(128, dtype=np.float32)
    inv = 10000.0 ** (-np.arange(128, dtype=np.float64) / 128.0)
    ang = inv[:, None] * np.arange(S, dtype=np.float64)[None, :]
    cs = np.stack([np.cos(ang), np.sin(ang)], axis=1).astype(np.float32)
    return rc, np.ascontiguousarray(cs)


_CACHE = {}


def kernel(**inputs):
    x = np.asarray(inputs["x"], np.float32)
    c = np.asarray(inputs["c"], np.float32)
    B = x.shape[0]
    rc, cs = _constants()
    rel_bias = np.asarray(inputs["rel_bias"], np.float32)
    kj = np.arange(128)[:, None]
    qq = np.arange(256)[None, :]
    m = np.clip(qq - kj, 0, 128)
    biasg = np.zeros((128, 12, 256), np.float32)
    for g, (win, dil) in enumerate(GROUPS):
        bidx = _t5_bucket_np(m * dil)
        for h in range(4):
            biasg[:, g * 4 + h, :] = rel_bias[bidx, g * 4 + h]
    vecs = np.concatenate([_col_layout(inputs["b_ada"][0]), _col_layout(inputs["norm1_g"][0]),
                           _col_layout(inputs["norm2_g"][0]), _col_layout(inputs["norm_f_g"]),
                           _col_layout(inputs["ret_gn_g"][0]), _col_layout(inputs["ret_gn_b"][0])], axis=1)
    shared = {
        "vecs": np.ascontiguousarray(vecs, np.float32),
        "w_ada": np.ascontiguousarray(inputs["w_ada"][0], np.float32),
        "w_in": np.ascontiguousarray(inputs["w_in"][0], np.float32),
        "w_ret_out": np.ascontiguousarray(inputs["w_ret_out"][0], np.float32),
        "w_att_out": np.ascontiguousarray(inputs["w_att_out"][0], np.float32),
        "w_o": np.ascontiguousarray(inputs["w_o"][0], np.float32),
        "w_ff1": np.ascontiguousarray(inputs["w_ff1"][0], np.float32),
        "w_ff2": np.ascontiguousarray(inputs["w_ff2"][0], np.float32),
        "biasg": biasg, "cossin": cs, "rconst": rc,
    }
    in_maps = []
    B = min(B, int(_CACHE.get("ncores", B)))
    for b in range(B):
        mp = dict(shared)
        mp["xT"] = np.ascontiguousarray(x[b].T)
        mp["ccol"] = _col_layout(c[b])
        in_maps.append(mp)
    debug = bool(_CACHE.get("debug", False))
    nc = build_program(debug=debug, stop=_CACHE.get("stop"), skip=_CACHE.get("skip", ()))
    res = run_bass_kernel_spmd(nc, in_maps, core_ids=list(range(B)))
    _CACHE["last"] = res
    out = np.stack([np.asarray(res.results[b]["outT"], np.float32).T for b in range(B)], axis=0)
    return np.ascontiguousarray(out)
```
